# Optimizing a Trainium2 kernel written in Bass

```python
import jax, jax.numpy as jnp
from jax import lax
import numpy as np

D_MODEL = 2048
BATCH = 2
SEQ = 4096
DEPTH = 4

RET_HEADS = 8
RET_QK_DIM = D_MODEL // 16
RET_V_DIM = 2 * RET_QK_DIM
RET_QK_WIDTH = RET_HEADS * RET_QK_DIM
RET_V_WIDTH = RET_HEADS * RET_V_DIM
RET_CHUNK = 128
RET_ROT_BASE = 10000.0
SWA_HEAD_DIM = 64
SWA_Q_HEADS = D_MODEL // SWA_HEAD_DIM
SWA_KV_HEADS = 8
SWA_Q_WIDTH = SWA_Q_HEADS * SWA_HEAD_DIM
SWA_KV_WIDTH = SWA_KV_HEADS * SWA_HEAD_DIM
SWA_WINDOW = 128
SWA_BLOCK = 128
IN_SPLITS = (RET_QK_WIDTH, RET_QK_WIDTH, RET_V_WIDTH, RET_V_WIDTH,
             SWA_Q_WIDTH, SWA_KV_WIDTH, SWA_KV_WIDTH, SWA_Q_WIDTH,
             D_MODEL, D_MODEL)
IN_WIDTH = sum(IN_SPLITS)
EPS = 1e-6

kernel_name = "hybrid_retention_swa_sink_adaln"


def rms_norm(x, w):
    xf = x.astype(jnp.float32)
    y = xf * lax.rsqrt(jnp.mean(xf * xf, axis=-1, keepdims=True) + EPS)
    return (y * w.astype(jnp.float32)).astype(x.dtype)


def rotate(x, pos):
    half = x.shape[-1] // 2
    inv = 1.0 / (RET_ROT_BASE ** jnp.linspace(0.0, 1.0, half, dtype=jnp.float32))
    ang = pos[:, None] * inv[None, :]
    cos = jnp.cos(ang)[None, :, None, :]
    sin = jnp.sin(ang)[None, :, None, :]
    x1, x2 = x[..., :half], x[..., half:]
    return jnp.concatenate([x1 * cos - x2 * sin, x2 * cos + x1 * sin], axis=-1)


def retention(q, k, v):
    b, s, h, _ = q.shape
    nc = s // RET_CHUNK
    log_g = jnp.log1p(-jnp.exp2(-5.0 - jnp.arange(h, dtype=jnp.float32)))
    idx = jnp.arange(RET_CHUNK, dtype=jnp.float32)
    rel = idx[:, None] - idx[None, :]
    decay = jnp.where(rel >= 0, jnp.exp(log_g[:, None, None] * jnp.maximum(rel, 0.0)), 0.0)
    xi = jnp.exp(log_g[None, :] * (idx[:, None] + 1.0))[None, :, :, None]
    zeta = jnp.exp(log_g[None, :] * (RET_CHUNK - 1.0 - idx[:, None]))[None, :, :, None]
    g_chunk = jnp.exp(log_g * RET_CHUNK)[None, :, None, None]

    def to_chunks(t):
        return jnp.moveaxis(t.reshape(b, nc, RET_CHUNK, h, t.shape[-1]), 1, 0)

    def step(state, inp):
        qc, kc, vc = inp
        scores = jnp.einsum('bnhd,bmhd->bhnm', qc, kc) * decay
        inner = jnp.einsum('bhnm,bmhe->bnhe', scores, vc)
        cross = jnp.einsum('bnhd,bhde->bnhe', qc, state) * xi
        new_state = state * g_chunk + jnp.einsum('bmhd,bmhe->bhde', kc * zeta, vc)
        return new_state, inner + cross

    s0 = jnp.zeros((b, h, q.shape[-1], v.shape[-1]), jnp.float32)
    _, out = lax.scan(step, s0, (to_chunks(q), to_chunks(k), to_chunks(v)))
    return jnp.moveaxis(out, 0, 1).reshape(b, s, h, v.shape[-1])


def sliding_window_attention(q, k, v, sinks):
    b, s, hq, dh = q.shape
    hkv = k.shape[2]
    g = hq // hkv
    nb = s // SWA_BLOCK
    qb = q.reshape(b, nb, SWA_BLOCK, hkv, g, dh)

    def band(t):
        tb = t.reshape(b, nb, SWA_BLOCK, hkv, dh)
        prev = jnp.pad(tb, ((0, 0), (1, 0), (0, 0), (0, 0), (0, 0)))[:, :-1]
        return jnp.concatenate([prev, tb], axis=2)

    kb, vb = band(k), band(v)
    scores = jnp.einsum('bnqhgd,bnkhd->bnhgqk', qb, kb).astype(jnp.float32) * (dh ** -0.5)
    qi = jnp.arange(SWA_BLOCK)[:, None]
    ki = jnp.arange(2 * SWA_BLOCK)[None, :]
    dist = qi + SWA_BLOCK - ki
    in_window = (dist >= 0) & (dist < SWA_WINDOW)
    not_first = (jnp.arange(nb) > 0)[:, None, None]
    allowed = in_window[None] & (not_first | (ki >= SWA_BLOCK)[None])
    scores = jnp.where(allowed[None, :, None, None], scores, -jnp.inf)
    sink = sinks.astype(jnp.float32).reshape(hkv, g)[None, None, :, :, None, None]
    m = jnp.maximum(jnp.max(scores, axis=-1, keepdims=True), sink)
    p = jnp.exp(scores - m)
    probs = (p / (jnp.sum(p, axis=-1, keepdims=True) + jnp.exp(sink - m))).astype(v.dtype)
    out = jnp.einsum('bnhgqk,bnkhd->bnqhgd', probs, vb)
    return out.reshape(b, s, hq * dh)


def hybrid_layer(x, c_act, pos, norm_w, ada_w, ada_b, w_in, ret_gn_w, sinks, w_ret_o, w_swa_o, w_out):
    b, s, _ = x.shape
    shift, scale, gate = jnp.split(c_act @ ada_w + ada_b, 3, axis=-1)
    u = rms_norm(x, norm_w) * (1.0 + scale[:, None, :]) + shift[:, None, :]
    proj = u @ w_in
    offsets = []
    acc = 0
    for w in IN_SPLITS[:-1]:
        acc += w
        offsets.append(acc)
    rq, rk, rv, rg, sq, sk, sv, sg, mg_ret, mg_swa = jnp.split(proj, offsets, axis=-1)

    rq = rotate(rq.reshape(b, s, RET_HEADS, RET_QK_DIM).astype(jnp.float32), pos)
    rk = rotate(rk.reshape(b, s, RET_HEADS, RET_QK_DIM).astype(jnp.float32), pos) * (RET_QK_DIM ** -0.5)
    rv = rv.reshape(b, s, RET_HEADS, RET_V_DIM).astype(jnp.float32)
    r = retention(rq, rk, rv)
    mu = jnp.mean(r, axis=-1, keepdims=True)
    var = jnp.mean(jnp.square(r - mu), axis=-1, keepdims=True)
    r = ((r - mu) * lax.rsqrt(var + EPS)).reshape(b, s, RET_V_WIDTH)
    r = (r * ret_gn_w.astype(jnp.float32)).astype(x.dtype) * jax.nn.silu(rg)
    ret_y = r @ w_ret_o

    a = sliding_window_attention(sq.reshape(b, s, SWA_Q_HEADS, SWA_HEAD_DIM),
                                 sk.reshape(b, s, SWA_KV_HEADS, SWA_HEAD_DIM),
                                 sv.reshape(b, s, SWA_KV_HEADS, SWA_HEAD_DIM), sinks)
    swa_y = (a * jax.nn.silu(sg)) @ w_swa_o

    merged = jax.nn.sigmoid(mg_ret) * ret_y + jax.nn.sigmoid(mg_swa) * swa_y
    return x + gate[:, None, :] * (merged @ w_out)


def setup_inputs(seed: int = 0) -> dict:
    key = jax.random.key(seed)
    ks = jax.random.split(key, 14)
    d = D_MODEL
    f32 = jnp.float32
    nrm = lambda k, shape: jax.random.normal(k, shape, f32)
    return {
        "x": nrm(ks[0], (BATCH, SEQ, d)),
        "c": nrm(ks[1], (BATCH, d)),
        "norm_w": 1.0 + 0.02 * nrm(ks[2], (DEPTH, d)),
        "ada_w": nrm(ks[3], (DEPTH, d, 3 * d)) * (0.5 * d ** -0.5),
        "ada_b": 0.01 * nrm(ks[4], (DEPTH, 3 * d)),
        "w_in": nrm(ks[5], (DEPTH, d, IN_WIDTH)) * (d ** -0.5),
        "ret_gn_w": 1.0 + 0.02 * nrm(ks[6], (DEPTH, RET_V_WIDTH)),
        "attn_sinks": nrm(ks[7], (DEPTH, SWA_Q_HEADS)),
        "w_ret_o": nrm(ks[8], (DEPTH, RET_V_WIDTH, d)) * (RET_V_WIDTH ** -0.5),
        "w_swa_o": nrm(ks[9], (DEPTH, SWA_Q_WIDTH, d)) * (SWA_Q_WIDTH ** -0.5),
        "w_out": nrm(ks[10], (DEPTH, d, d)) * (d ** -0.5),
        "final_norm_w": 1.0 + 0.02 * nrm(ks[11], (d,)),
    }


def reference(x, c, norm_w, ada_w, ada_b, w_in, ret_gn_w, attn_sinks, w_ret_o, w_swa_o, w_out, final_norm_w):
    c_act = jax.nn.silu(c)
    pos = jnp.arange(x.shape[1], dtype=jnp.float32)
    h = x
    for l in range(DEPTH):
        h = hybrid_layer(h, c_act, pos, norm_w[l], ada_w[l], ada_b[l], w_in[l], ret_gn_w[l],
                         attn_sinks[l], w_ret_o[l], w_swa_o[l], w_out[l])
    return rms_norm(h, final_norm_w)
```

```python
import numpy as np
import concourse.bass as bass
import concourse.mybir as mybir
from concourse.bass_utils import run_bass_kernel_spmd

F32 = mybir.dt.float32
BF16 = mybir.dt.bfloat16
AF = mybir.ActivationFunctionType
ALU = mybir.AluOpType
AX = mybir.AxisListType

D = 2048
KC = 16
T = 1024
NT = 8
DEPTH = 4
NCORES = 8
EPS = 1e-6
NSLAB = 120
RING = 4
import os
DBG = int(os.environ.get('KDBG', '9'))
O_RQ, O_RK, O_RV, O_RG, O_SQ, O_SK, O_SV, O_SG, O_MR, O_MS = (
    0, 1024, 2048, 4096, 6144, 8192, 8704, 9216, 11264, 13312)

TB_COS, TB_SIN, TB_XI, TB_KZ, TB_G, TB_GPOW, TB_COEF, TB_SEL = 0, 512, 1024, 1032, 1040, 1048, 1112, 1176
NTAB = 1184
PV_NW, PV_AB, PV_GN, PV_FW = 0, 64, 256, 320
NPV = 336


class Sched:
    ENGS = ("pe", "act", "dve", "pool", "sp")

    def __init__(self, ndma_w=8, ndma_m=8):
        self.q = {e: [] for e in self.ENGS}
        self.cnt = {e: 0 for e in self.ENGS}
        self.waited = {e: {} for e in self.ENGS}
        self.lastw = {}
        self.readers = {}
        self.dsem = {"w": ["dw%d" % i for i in range(ndma_w)], "m": ["dm%d" % i for i in range(ndma_m)]}
        self.dcnt = {n: 0 for k in self.dsem for n in self.dsem[k]}
        self.dcnt["cc"] = 0
        self.drr = {"w": 0, "m": 0}
        self.sem_names = list(self.ENGS) + list(self.dcnt.keys())

    def _deps(self, reads, writes):
        toks = []
        for k in reads:
            if k in self.lastw:
                toks.append(self.lastw[k])
        for k in writes:
            if k in self.lastw:
                toks.append(self.lastw[k])
            r = self.readers.get(k)
            if r:
                toks.extend(r.items())
        return toks

    def _waits(self, eng, toks):
        need = {}
        for sem, val in toks:
            if val > need.get(sem, 0):
                need[sem] = val
        w = self.waited[eng]
        for sem, val in need.items():
            if eng == "pe" and sem == "pe":
                continue
            if w.get(sem, 0) >= val:
                continue
            w[sem] = val
            self.q[eng].append(("wait", sem, val))

    def _record(self, tok, reads, writes):
        for k in reads:
            r = self.readers.setdefault(k, {})
            if tok[1] > r.get(tok[0], 0):
                r[tok[0]] = tok[1]
        for k in writes:
            self.lastw[k] = tok
            self.readers[k] = {}

    def op(self, eng, fn, reads=(), writes=()):
        self._waits(eng, self._deps(reads, writes))
        self.cnt[eng] += 1
        tok = (eng, self.cnt[eng])
        self.q[eng].append(("op", fn, eng, 1))
        self._record(tok, reads, writes)
        return tok

    def dma(self, issuer, fn, reads=(), writes=(), kind="m"):
        names = self.dsem[kind]
        ds = names[self.drr[kind] % len(names)]
        self.drr[kind] += 1
        toks = self._deps(reads, writes)
        if self.dcnt[ds] > 0:
            toks.append((ds, self.dcnt[ds]))
        self._waits(issuer, toks)
        self.dcnt[ds] += 16
        tok = (ds, self.dcnt[ds])
        self.q[issuer].append(("op", fn, ds, 16))
        self._record(tok, reads, writes)
        return tok

    def collective(self, fn, reads=(), writes=()):
        toks = self._deps(reads, writes)
        self._waits("pool", toks)
        self.dcnt["cc"] += 1
        tok = ("cc", self.dcnt["cc"])
        self.q["pool"].append(("op", fn, "cc", 1))
        self._record(tok, reads, writes)
        return tok

    def barrier(self):
        toks = [(e, self.cnt[e]) for e in self.ENGS if self.cnt[e] > 0]
        toks += [(n, v) for n, v in self.dcnt.items() if v > 0]
        for e in self.ENGS:
            self._waits(e, [t for t in toks if t[0] != e or e != "pe"])

    def emit(self, eng, handle, sems):
        for it in self.q[eng]:
            if it[0] == "wait":
                handle.wait_ge(sems[it[1]], it[2])
            else:
                inst = it[1](handle)
                inst.then_inc(sems[it[2]], it[3])


def pk(c0, c1):
    return [("ps", q) for q in range(c0 // 512, (c1 + 511) // 512)]


def build_program(l0, l1, final):
    nl = l1 - l0
    nc = bass.Bass("TRN2", target_bir_lowering=False)
    x_in = nc.dram_tensor("x", [T, D], F32, kind="ExternalInput").ap()
    wsl = nc.dram_tensor("wslab", [nl * NSLAB, 128, 4096], F32, kind="ExternalInput").ap()
    cvec = nc.dram_tensor("cvec", [128, KC], F32, kind="ExternalInput").ap()
    pvec = nc.dram_tensor("pvec", [128, NPV], F32, kind="ExternalInput").ap()
    sinks_in = nc.dram_tensor("sinks", [128, 128], F32, kind="ExternalInput").ap()
    tabs_in = nc.dram_tensor("tabs", [128, NTAB], F32, kind="ExternalInput").ap()
    masks_in = nc.dram_tensor("masks", [128, 768], F32, kind="ExternalInput").ap()
    y_out = nc.dram_tensor("y", [T, D], F32, kind="ExternalOutput").ap()
    g_in = [[nc.dram_tensor("gin_%d_%d" % (l, h), [128, 256], F32) for h in range(8)] for l in range(nl)]
    g_out = [[nc.dram_tensor("gout_%d_%d" % (l, h), [NCORES * 128, 256], F32) for h in range(8)] for l in range(nl)]
    h_in = [[nc.dram_tensor("hin_%d_%d" % (l, h), [128, 192], BF16) for h in range(8)] for l in range(nl)]
    h_out = [[nc.dram_tensor("hout_%d_%d" % (l, h), [NCORES * 128, 192], BF16) for h in range(8)] for l in range(nl)]

    S = Sched()
    ctx = []

    def sb(name, shape, dt):
        g = nc.sbuf_tensor("sb_" + name, shape, dt)
        ctx.append(g)
        return g.__enter__()

    xT = sb("xT", [128, KC, T], F32)
    uT = sb("uT", [128, KC, T], BF16)
    brT = sb("brT", [128, KC, T], BF16)
    ring = sb("ring", [128, RING, 4096], BF16)
    PRN = 18048
    PR = sb("PR", [128, PRN], BF16)
    tabs = sb("tabs", [128, NTAB], F32)
    pv = sb("pv", [128, NPV], F32)
    sinks = sb("sinks", [128, 128], F32)
    cv = sb("cv", [128, KC], F32)
    cact = sb("cact", [128, KC], BF16)
    ident_f = sb("ident_f", [128, 128], F32)
    ident_b = sb("ident_b", [128, 128], BF16)
    ones_b = sb("ones_b", [128, 128], BF16)
    causal_b = sb("causal_b", [128, 128], BF16)
    maskA = sb("maskA", [128, 256], BF16)
    maskB = sb("maskB", [128, 256], BF16)
    modsb = sb("modsb", [128, 48], F32)
    asb = sb("asb", [128, KC], F32)
    small = sb("small", [128, 96], F32)
    pg = nc.psum_tensor("pbig", [128, 4096], F32)
    ctx.append(pg)
    PB = pg.__enter__()
    PBh = PB[:, :].bitcast(BF16)

    def psf(c0, c1):
        return PB[:, c0:c1]

    def psb(c0, c1):
        return PBh[:, 2 * c0:2 * c1]

    class Carve:
        def __init__(self):
            self.off = 0

        def bf(self, n):
            a = PR[:, self.off:self.off + n]
            self.off += n
            assert self.off <= PRN, self.off
            return a

        def f32(self, n):
            a = PR[:, self.off:self.off + 2 * n].bitcast(F32)
            self.off += 2 * n
            assert self.off <= PRN, self.off
            return a

    act_ = lambda out, in_, func, **kw: (lambda e: e.activation(out=out, in_=in_, func=func, **kw))
    tt_ = lambda out, a, b, op: (lambda e: e.tensor_tensor(out=out, in0=a, in1=b, op=op))
    ts_ = lambda out, a, s1, s2, op0, op1=None: (
        (lambda e: e.tensor_scalar(out=out, in0=a, scalar1=s1, scalar2=s2, op0=op0, op1=op1)) if op1 is not None
        else (lambda e: e.tensor_scalar(out=out, in0=a, scalar1=s1, scalar2=None, op0=op0)))
    stt_ = lambda out, a, s, b, op0, op1: (lambda e: e.scalar_tensor_tensor(out=out, in0=a, scalar=s, in1=b, op0=op0, op1=op1))
    cp_ = lambda out, in_: (lambda e: e.tensor_copy(out=out, in_=in_))

    def mmgroup(mms):
        def f(pe):
            inst = None
            for (o, l, r, st, sp) in mms:
                inst = pe.matmul(o, lhsT=l, rhs=r, start=st, stop=sp)
            return inst
        return f

    def trgroup(trs):
        def f(pe):
            inst = None
            for (o, i, idn) in trs:
                inst = pe.transpose(o, i, idn)
            return inst
        return f

    wstate = {"issued": 0}
    total_slabs = nl * NSLAB

    def slab_issue_upto(n):
        n = min(n, total_slabs - 1)
        while wstate["issued"] <= n:
            i = wstate["issued"]
            slot = i % RING
            S.dma("pool", (lambda e, i=i, slot=slot: e.dma_start(out=ring[:, slot, :], in_=wsl[i], max_dma_last_dim=8192)),
                  writes=[("ring", slot)], kind="w")
            wstate["issued"] += 1

    def slab(i):
        slab_issue_upto(i + RING - 2)
        slot = i % RING
        return ring[:, slot, :].rearrange("p (k c) -> p k c", k=KC), ("ring", slot)

    S.dma("sp", lambda e: e.dma_start(out=tabs[:], in_=tabs_in), writes=["tabs"])
    S.dma("sp", lambda e: e.dma_start(out=pv[:], in_=pvec), writes=["pv"])
    S.dma("sp", lambda e: e.dma_start(out=sinks[:], in_=sinks_in), writes=["sinks"])
    S.dma("sp", lambda e: e.dma_start(out=cv[:], in_=cvec), writes=["cv"])
    cz = Carve()
    mstage = cz.f32(768)
    stage = [cz.f32(2048), cz.f32(2048)]
    S.dma("sp", lambda e: e.dma_start(out=mstage, in_=masks_in), writes=["mstage"])
    S.op("dve", cp_(ident_f[:], mstage[:, 0:128]), reads=["mstage"], writes=["ident_f"])
    S.op("dve", cp_(ident_b[:], mstage[:, 0:128]), reads=["mstage"], writes=["ident_b"])
    S.op("dve", cp_(causal_b[:], mstage[:, 128:256]), reads=["mstage"], writes=["causal_b"])
    S.op("dve", cp_(maskA[:], mstage[:, 256:512]), reads=["mstage"], writes=["maskA"])
    S.op("dve", cp_(maskB[:], mstage[:, 512:768]), reads=["mstage"], writes=["maskB"])
    S.op("pool", lambda e: e.memset(ones_b[:], 1.0), writes=["ones_b"])
    S.op("pool", lambda e: e.memset(small[:, 0:1], EPS), writes=["epsc"])
    epsc = small[:, 0:1]
    S.op("act", act_(cact[:], cv[:], AF.Silu), reads=["cv"], writes=["cact"])
    slab_issue_upto(RING - 1)

    for n in range(NT):
        st = stage[n % 2]
        S.dma("sp", (lambda e, n=n, st=st: e.dma_start(out=st, in_=x_in[n * 128:(n + 1) * 128, :])),
              writes=[("stage", n % 2)])
        for g4 in range(4):
            bank = (n * 4 + g4) % 2
            c0 = bank * 512
            trs = [(psf(c0 + j * 128, c0 + (j + 1) * 128), st[:, (g4 * 4 + j) * 128:(g4 * 4 + j + 1) * 128], ident_f[:])
                   for j in range(4)]
            S.op("pe", trgroup(trs), reads=[("stage", n % 2), "ident_f"], writes=pk(c0, c0 + 512))
            eng = "act" if g4 % 2 == 0 else "dve"
            outv = xT[:, g4 * 4:(g4 + 1) * 4, n * 128:(n + 1) * 128]
            inv = psf(c0, c0 + 512).rearrange("p (j t) -> p j t", j=4)
            fn = (lambda e, o=outv, i=inv: e.activation(out=o, in_=i, func=AF.Copy)) if eng == "act" else cp_(outv, inv)
            S.op(eng, fn, reads=pk(c0, c0 + 512), writes=[("xT", k, n // 4) for k in range(g4 * 4, g4 * 4 + 4)])
    S.barrier()

    def norm_phase(wcol_ap, with_shift, dst_is_u):
        cz = Carve()
        rstd = cz.f32(1024)
        sq = [cz.bf(512), cz.bf(512)]
        xn = [cz.f32(512), cz.f32(512)]
        for th in range(2):
            c0 = (3 + th) * 512
            for k in range(KC):
                s_ = sq[k % 2]
                S.op("act", act_(s_, xT[:, k, th * 512:(th + 1) * 512], AF.Square), reads=[("xT", k, th)], writes=[("sq", k % 2)])
                S.op("pe", mmgroup([(psf(c0, c0 + 512), ones_b[:], s_, k == 0, k == KC - 1)]),
                     reads=[("sq", k % 2), "ones_b"], writes=pk(c0, c0 + 512))
            rs = rstd[:, th * 512:(th + 1) * 512]
            S.op("act", act_(rs, psf(c0, c0 + 512), AF.Sqrt, scale=1.0 / D, bias=epsc), reads=pk(c0, c0 + 512) + ["epsc"], writes=[("rstd", th)])
            S.op("dve", (lambda e, rs=rs: e.reciprocal(out=rs, in_=rs)), reads=[("rstd", th)], writes=[("rstd", th)])
        i = 0
        for th in range(2):
            for k in range(KC):
                rs = rstd[:, th * 512:(th + 1) * 512]
                xs = xT[:, k, th * 512:(th + 1) * 512]
                t_ = xn[i % 2]
                eng = "dve" if i % 2 == 0 else "pool"
                S.op(eng, tt_(t_, xs, rs, ALU.mult), reads=[("xT", k, th), ("rstd", th)], writes=[("xn", i % 2)])
                if dst_is_u:
                    S.op("act", act_(uT[:, k, th * 512:(th + 1) * 512], t_, AF.Identity, scale=asb[:, k:k + 1], bias=with_shift[:, k:k + 1]),
                         reads=[("xn", i % 2), "asb", "modsb"], writes=[("uT", k, th)])
                else:
                    S.op("act", act_(xs, t_, AF.Identity, scale=wcol_ap[:, k:k + 1]), reads=[("xn", i % 2), "pv"], writes=[("xT", k, th)])
                i += 1

    for li in range(nl):
        l = l0 + li
        sbase = li * NSLAB
        mc0 = 2 * 512
        for i in (range(24) if DBG >= 1 else []):
            w, wk = slab(sbase + i)
            mms = []
            for cc in range(2):
                col = mc0 + i * 2 + cc
                for k in range(KC):
                    mms.append((psf(col, col + 1), w[:, k, cc * 128:(cc + 1) * 128], cact[:, k:k + 1], k == 0, k == KC - 1))
            S.op("pe", mmgroup(mms), reads=[wk, "cact"], writes=pk(mc0, mc0 + 48))
        S.op("dve", tt_(modsb[:], psf(mc0, mc0 + 48), pv[:, PV_AB + li * 48:PV_AB + (li + 1) * 48], ALU.add),
             reads=pk(mc0, mc0 + 48) + ["pv"], writes=["modsb"])
        S.op("dve", stt_(asb[:], modsb[:, 16:32], 1.0, pv[:, PV_NW + li * 16:PV_NW + (li + 1) * 16], ALU.add, ALU.mult),
             reads=["modsb", "pv"], writes=["asb"])
        shift = modsb[:, 0:16]
        gate = modsb[:, 32:48]
        if DBG >= 1:
            norm_phase(None, shift, True)
        S.barrier()

        for branch in range(2):
            if (branch == 0 and DBG < 2) or (branch == 1 and DBG < 4):
                continue
            if branch == 0:
                cz = Carve()
                qkf = [cz.f32(256), cz.f32(256)]
                t1 = cz.f32(256); t2 = cz.f32(256); ssum = cz.f32(256)
                qhat = [cz.bf(128), cz.bf(128)]
                khat = cz.bf(1024)
                vtok = cz.bf(2048)
                qkT = cz.bf(2048)
                siluG = cz.bf(2048)
                Sloc = cz.f32(2048)
                Sbf = [cz.bf(256), cz.bf(256)]
                gath = [cz.f32(256), cz.f32(256)]
                Sin = cz.f32(256)
                Gn = cz.f32(256)
                lend = cz.f32(256)
                Pm = [cz.bf(128), cz.bf(128)]
                rhat = [cz.bf(256), cz.bf(256)]
                st6 = [cz.f32(6), cz.f32(6)]
                mv = [cz.f32(2), cz.f32(2)]
                rsr = [cz.f32(2), cz.f32(2)]
                qkT3 = qkT.rearrange("p (a t) -> p a t", a=2)
                sG3 = siluG.rearrange("p (a t) -> p a t", a=2)
                for h in range(8):
                    wqk, kqk = slab(sbase + 24 + 3 * h)
                    wv, kv_ = slab(sbase + 24 + 3 * h + 1)
                    xi = tabs[:, TB_XI + h:TB_XI + h + 1]
                    kz = tabs[:, TB_KZ + h:TB_KZ + h + 1]
                    gh = tabs[:, TB_G + h:TB_G + h + 1]
                    for n in range(NT):
                        tk = slice(n * 128, (n + 1) * 128)
                        th = n // 4
                        c0 = (n % 2) * 512
                        mms = [(psf(c0, c0 + 256), uT[:, k, tk], wqk[:, k, :], k == 0, k == KC - 1) for k in range(KC)]
                        mms += [(psf(c0 + 256, c0 + 512), uT[:, k, tk], wv[:, k, :], k == 0, k == KC - 1) for k in range(KC)]
                        S.op("pe", mmgroup(mms), reads=[kqk, kv_] + [("uT", k, th) for k in range(KC)], writes=pk(c0, c0 + 512))
                        qf = qkf[n % 2]
                        S.op("act", act_(qf, psf(c0, c0 + 256), AF.Copy), reads=pk(c0, c0 + 256), writes=[("qkf", n % 2)])
                        S.op("act", act_(vtok[:, n * 256:(n + 1) * 256], psf(c0 + 256, c0 + 512), AF.Copy),
                             reads=pk(c0 + 256, c0 + 512), writes=[("vtok", n)])
                        q4 = qf.rearrange("p (a b d) -> p a b d", a=2, b=2)
                        t14 = t1.rearrange("p (a b d) -> p a b d", a=2, b=2)
                        t24 = t2.rearrange("p (a b d) -> p a b d", a=2, b=2)
                        s4 = ssum.rearrange("p (a b d) -> p a b d", a=2, b=2)
                        cosn = tabs[:, TB_COS + n * 64:TB_COS + (n + 1) * 64]
                        sinn = tabs[:, TB_SIN + n * 64:TB_SIN + (n + 1) * 64]
                        cos4 = cosn.unsqueeze(1).unsqueeze(1).broadcast_to([128, 2, 2, 64])
                        sin3 = sinn.unsqueeze(1).broadcast_to([128, 2, 64])
                        S.op("pool", tt_(t14, q4, cos4, ALU.mult), reads=[("qkf", n % 2), "tabs"], writes=["t1"])
                        S.op("pool", tt_(t24[:, :, 0, :], q4[:, :, 1, :], sin3, ALU.mult), reads=[("qkf", n % 2), "tabs"], writes=["t2a"])
                        S.op("pool", tt_(t24[:, :, 1, :], q4[:, :, 0, :], sin3, ALU.mult), reads=[("qkf", n % 2), "tabs"], writes=["t2b"])
                        S.op("pool", tt_(s4[:, :, 0, :], t14[:, :, 0, :], t24[:, :, 0, :], ALU.subtract), reads=["t1", "t2a"], writes=["ssa"])
                        S.op("pool", tt_(s4[:, :, 1, :], t14[:, :, 1, :], t24[:, :, 1, :], ALU.add), reads=["t1", "t2b"], writes=["ssb"])
                        qh_ = qhat[n % 2]
                        S.op("act", act_(qh_, ssum[:, 0:128], AF.Identity, scale=xi), reads=["ssa", "ssb", "tabs"], writes=[("qhat", n % 2)])
                        S.op("act", act_(khat[:, tk], ssum[:, 128:256], AF.Identity, scale=kz), reads=["ssa", "ssb", "tabs"], writes=[("khat", n)])
                        tb = (4 + n % 2) * 512
                        tp = psb(tb, tb + 128)
                        S.op("pe", trgroup([(tp[:, 0:128], qh_, ident_b[:]), (tp[:, 128:256], khat[:, tk], ident_b[:])]),
                             reads=[("qhat", n % 2), ("khat", n), "ident_b"], writes=pk(tb, tb + 128))
                        S.op("dve", cp_(qkT3[:, :, tk], tp.rearrange("p (a t) -> p a t", a=2)), reads=pk(tb, tb + 128), writes=[("qkT", n)])
                    for n in range(NT):
                        tk = slice(n * 128, (n + 1) * 128)
                        kc0 = (6 + n % 2) * 512
                        S.op("pe", mmgroup([(psf(kc0, kc0 + 256), khat[:, tk], vtok[:, n * 256:(n + 1) * 256], True, True)]),
                             reads=[("khat", n), ("vtok", n)], writes=pk(kc0, kc0 + 256))
                        if n == 0:
                            S.op("dve", cp_(Sloc[:, 0:256], psf(kc0, kc0 + 256)), reads=pk(kc0, kc0 + 256), writes=[("Sloc", 0)])
                        else:
                            S.op("dve", stt_(Sloc[:, n * 256:(n + 1) * 256], Sloc[:, (n - 1) * 256:n * 256], gh, psf(kc0, kc0 + 256), ALU.mult, ALU.add),
                                 reads=pk(kc0, kc0 + 256) + [("Sloc", n - 1), "tabs"], writes=[("Sloc", n)])
                    S.op("act", act_(lend, Sloc[:, 7 * 256:8 * 256], AF.Identity, scale=gh), reads=[("Sloc", 7), "tabs"], writes=["lend"])
                    gi = g_in[li][h]
                    go = g_out[li][h]
                    S.dma("sp", (lambda e, gi=gi: e.dma_start(out=gi.ap(), in_=lend)), reads=["lend"], writes=[("gin", li, h)])
                    S.collective((lambda e, gi=gi, go=go: e.collective_compute(
                        "AllGather", ALU.bypass, replica_groups=[list(range(NCORES))], ins=[gi.ap().opt()], outs=[go.ap().opt()])),
                        reads=[("gin", li, h)], writes=[("gout", li, h)])
                    wg, kg = slab(sbase + 24 + 3 * h + 2)
                    for fc in range(2):
                        for th in range(2):
                            gc0 = (2 + (fc * 2 + th) % 2) * 512
                            mms = [(psf(gc0, gc0 + 512), wg[:, k, fc * 128:(fc + 1) * 128], uT[:, k, th * 512:(th + 1) * 512], k == 0, k == KC - 1)
                                   for k in range(KC)]
                            S.op("pe", mmgroup(mms), reads=[kg] + [("uT", k, th) for k in range(KC)], writes=pk(gc0, gc0 + 512))
                            S.op("act", act_(sG3[:, fc, th * 512:(th + 1) * 512], psf(gc0, gc0 + 512), AF.Silu),
                                 reads=pk(gc0, gc0 + 512), writes=[("siluG", fc, th)])
                    for i in range(NCORES):
                        gb = gath[i % 2]
                        S.dma("sp", (lambda e, go=go, gb=gb, i=i: e.dma_start(out=gb, in_=go.ap()[i * 128:(i + 1) * 128, :])),
                              reads=[("gout", li, h)], writes=[("gath", i % 2)])
                        cf = tabs[:, TB_COEF + i * 8 + h:TB_COEF + i * 8 + h + 1]
                        if i == 0:
                            S.op("dve", ts_(Sin, gb, cf, None, ALU.mult), reads=[("gath", 0), "tabs"], writes=["Sin"])
                        else:
                            S.op("dve", stt_(Sin, gb, cf, Sin, ALU.mult, ALU.add), reads=[("gath", i % 2), "tabs", "Sin"], writes=["Sin"])
                    for n in range(NT):
                        tk = slice(n * 128, (n + 1) * 128)
                        sb_ = Sbf[n % 2]
                        if n == 0:
                            S.op("act", act_(sb_, Sin, AF.Copy), reads=["Sin"], writes=[("Sbf", 0)])
                        else:
                            gp = tabs[:, TB_GPOW + n * 8 + h:TB_GPOW + n * 8 + h + 1]
                            S.op("act", act_(Gn, Sin, AF.Identity, scale=gp), reads=["Sin", "tabs"], writes=["Gn"])
                            S.op("dve", stt_(sb_, Sloc[:, (n - 1) * 256:n * 256], gh, Gn, ALU.mult, ALU.add),
                                 reads=[("Sloc", n - 1), "Gn", "tabs"], writes=[("Sbf", n % 2)])
                        sc0 = (4 + n % 2) * 512 + 128
                        S.op("pe", mmgroup([(psf(sc0, sc0 + 128), qkT3[:, 1, tk], qkT3[:, 0, tk], True, True)]),
                             reads=[("qkT", n)], writes=pk(sc0, sc0 + 128))
                        pm_ = Pm[n % 2]
                        S.op("dve", tt_(pm_, psf(sc0, sc0 + 128), causal_b[:], ALU.mult), reads=pk(sc0, sc0 + 128) + ["causal_b"], writes=[("Pm", n % 2)])
                        oc0 = (6 + n % 2) * 512
                        S.op("pe", mmgroup([(psf(oc0, oc0 + 256), pm_, vtok[:, n * 256:(n + 1) * 256], True, False),
                                            (psf(oc0, oc0 + 256), qkT3[:, 0, tk], sb_, False, True)]),
                             reads=[("Pm", n % 2), ("vtok", n), ("qkT", n), ("Sbf", n % 2)], writes=pk(oc0, oc0 + 256))
                        s6 = st6[n % 2]; m2 = mv[n % 2]; r1 = rsr[n % 2]; rh = rhat[n % 2]
                        S.op("dve", (lambda e, s6=s6, oc0=oc0: e.bn_stats(out=s6, in_=psf(oc0, oc0 + 256))), reads=pk(oc0, oc0 + 256), writes=[("st6", n % 2)])
                        S.op("dve", (lambda e, s6=s6, m2=m2: e.bn_aggr(out=m2, in_=s6)), reads=[("st6", n % 2)], writes=[("mv", n % 2)])
                        S.op("act", act_(r1[:, 0:1], m2[:, 1:2], AF.Sqrt, scale=1.0, bias=epsc), reads=[("mv", n % 2), "epsc"], writes=[("rsr", n % 2)])
                        S.op("dve", (lambda e, r1=r1: e.reciprocal(out=r1[:, 0:1], in_=r1[:, 0:1])), reads=[("rsr", n % 2)], writes=[("rsr", n % 2)])
                        S.op("dve", stt_(r1[:, 1:2], m2[:, 0:1], -1.0, r1[:, 0:1], ALU.mult, ALU.mult), reads=[("mv", n % 2), ("rsr", n % 2)], writes=[("nmr", n % 2)])
                        S.op("act", act_(rh, psf(oc0, oc0 + 256), AF.Identity, scale=r1[:, 0:1], bias=r1[:, 1:2]),
                             reads=pk(oc0, oc0 + 256) + [("nmr", n % 2), ("rsr", n % 2)], writes=[("rhat", n % 2)])
                        rb = (n % 2) * 512
                        rp = psb(rb, rb + 128)
                        S.op("pe", trgroup([(rp[:, 0:128], rh[:, 0:128], ident_b[:]), (rp[:, 128:256], rh[:, 128:256], ident_b[:])]),
                             reads=[("rhat", n % 2), "ident_b"], writes=pk(rb, rb + 128))
                        for i in range(2):
                            f = h * 2 + i
                            gw = pv[:, PV_GN + li * 16 + f:PV_GN + li * 16 + f + 1]
                            S.op("dve", stt_(brT[:, f, tk], rp[:, i * 128:(i + 1) * 128], gw, sG3[:, i, tk], ALU.mult, ALU.mult),
                                 reads=pk(rb, rb + 128) + [("siluG", i, n // 4), "pv"], writes=[("brT", f, n // 4)])
            else:
                cz = Carve()
                kTs = cz.bf(1152)
                vS = cz.bf(9 * 66)
                qTs = cz.bf(4096)
                siluG = cz.bf(2048)
                hg = cz.bf(8 * 192)
                hacc = cz.f32(192)
                E = [cz.bf(1024), cz.bf(1024)]
                Pm = [cz.bf(1024), cz.bf(1024)]
                PTs = [cz.bf(1024), cz.bf(1024)]
                an = [cz.bf(256), cz.bf(256)]
                sm = [cz.f32(32), cz.f32(32)]
                vS3 = vS.rearrange("p (b c) -> p b c", c=66)
                qT3 = qTs.rearrange("p (a t) -> p a t", a=4)
                sG3 = siluG.rearrange("p (a t) -> p a t", a=2)
                hg3 = hg.rearrange("p (r c) -> p r c", c=192)
                S.op("pool", lambda e: e.memset(vS3[:, :, 64:66], 1.0), writes=["vS1"])
                S.op("pool", lambda e: e.memset(qTs, 0.0), writes=[("qTs", fc, th) for fc in range(2) for th in range(2)])
                for h in range(8):
                    wkv, kkv = slab(sbase + 72 + 3 * h)
                    for th in range(2):
                        c0 = (th % 2) * 512
                        mms = [(psf(c0, c0 + 512), wkv[:, k, 0:128], uT[:, k, th * 512:(th + 1) * 512], k == 0, k == KC - 1) for k in range(KC)]
                        S.op("pe", mmgroup(mms), reads=[kkv] + [("uT", k, th) for k in range(KC)], writes=pk(c0, c0 + 512))
                        S.op("act", act_(kTs[:, 128 + th * 512:128 + (th + 1) * 512], psf(c0, c0 + 512), AF.Copy),
                             reads=pk(c0, c0 + 512), writes=[("kTs", 1 + th)])
                    vc0 = 2 * 512
                    mms = []
                    for n in range(NT):
                        mms += [(psf(vc0 + n * 64, vc0 + (n + 1) * 64), uT[:, k, n * 128:(n + 1) * 128], wkv[:, k, 128:192], k == 0, k == KC - 1)
                                for k in range(KC)]
                    S.op("pe", mmgroup(mms), reads=[kkv] + [("uT", k, th) for k in range(KC) for th in range(2)], writes=pk(vc0, vc0 + 512))
                    S.op("dve", cp_(vS3[:, 1:9, 0:64], psf(vc0, vc0 + 512).rearrange("p (b c) -> p b c", c=64)),
                         reads=pk(vc0, vc0 + 512), writes=["vS"])
                    hi = h_in[li][h]
                    ho = h_out[li][h]
                    S.dma("sp", (lambda e, hi=hi: e.dma_start(out=hi.ap()[:, 0:128], in_=kTs[:, 1024:1152])), reads=[("kTs", 2)], writes=[("hin", li, h, 0)])
                    S.dma("sp", (lambda e, hi=hi: e.dma_start(out=hi.ap()[:, 128:192], in_=vS3[:, 8, 0:64])), reads=["vS"], writes=[("hin", li, h, 1)])
                    S.collective((lambda e, hi=hi, ho=ho: e.collective_compute(
                        "AllGather", ALU.bypass, replica_groups=[list(range(NCORES))], ins=[hi.ap().opt()], outs=[ho.ap().opt()])),
                        reads=[("hin", li, h, 0), ("hin", li, h, 1)], writes=[("hout", li, h)])
                    for is_gate in (False, True):
                        wsrc, wkey = slab(sbase + 72 + 3 * h + (2 if is_gate else 1))
                        for fc in range(2):
                            for th in range(2):
                                c0 = ((fc * 2 + th) % 2) * 512
                                mms = [(psf(c0, c0 + 512), wsrc[:, k, fc * 128:(fc + 1) * 128], uT[:, k, th * 512:(th + 1) * 512], k == 0, k == KC - 1)
                                       for k in range(KC)]
                                S.op("pe", mmgroup(mms), reads=[wkey] + [("uT", k, th) for k in range(KC)], writes=pk(c0, c0 + 512))
                                if is_gate:
                                    S.op("act", act_(sG3[:, fc, th * 512:(th + 1) * 512], psf(c0, c0 + 512), AF.Silu),
                                         reads=pk(c0, c0 + 512), writes=[("siluG", fc, th)])
                                else:
                                    for hh in range(2):
                                        S.op("act", act_(qT3[hh * 64:(hh + 1) * 64, fc * 2 + hh, th * 512:(th + 1) * 512], psf(c0, c0 + 512)[hh * 64:(hh + 1) * 64, :], AF.Copy),
                                             reads=pk(c0, c0 + 512), writes=[("qTs", fc, th)])
                    S.dma("sp", (lambda e, ho=ho: e.dma_start(out=hg3, in_=ho.ap().rearrange("(r p) f -> p r f", p=128))),
                          reads=[("hout", li, h)], writes=["hg"])
                    for i in range(NCORES):
                        sl = tabs[:, TB_SEL + i:TB_SEL + i + 1]
                        if i == 0:
                            S.op("dve", ts_(hacc, hg3[:, 0, :], sl, None, ALU.mult), reads=["hg", "tabs"], writes=["hacc"])
                        else:
                            S.op("dve", stt_(hacc, hg3[:, i, :], sl, hacc, ALU.mult, ALU.add), reads=["hg", "tabs", "hacc"], writes=["hacc"])
                    S.op("dve", cp_(kTs[:, 0:128], hacc[:, 0:128]), reads=["hacc"], writes=[("kTs", 0)])
                    S.op("dve", cp_(vS3[:, 0, 0:64], hacc[:, 128:192]), reads=["hacc"], writes=["vS0"])
                    for n in range(NT):
                        tk = slice(n * 128, (n + 1) * 128)
                        sc0 = (3 + 2 * (n % 2)) * 512
                        mms = []
                        for qh in range(4):
                            fc = qh // 2
                            pr = (qh % 2) * 64
                            mms.append((psf(sc0 + qh * 256, sc0 + (qh + 1) * 256), qT3[:, qh, tk], kTs[:, n * 128:n * 128 + 256], True, True))
                        kkeys = [("kTs", 0), ("kTs", 1), ("kTs", 2)]
                        S.op("pe", mmgroup(mms), reads=kkeys + [("qTs", fc, n // 4) for fc in range(2)], writes=pk(sc0, sc0 + 1024))
                        s_ = sm[n % 2]
                        mx, mm_, nm, tmp, es, den, rinv = (s_[:, 0:4], s_[:, 4:8], s_[:, 8:12], s_[:, 12:16], s_[:, 16:20], s_[:, 20:24], s_[:, 24:28])
                        smk = ("sm", n % 2)
                        sc3 = psf(sc0, sc0 + 1024).rearrange("p (q k) -> p q k", q=4)
                        snk = sinks[:, li * 32 + 4 * h:li * 32 + 4 * h + 4]
                        S.op("dve", (lambda e, mx=mx, sc3=sc3: e.tensor_reduce(out=mx, in_=sc3, axis=AX.X, op=ALU.max)),
                             reads=pk(sc0, sc0 + 1024), writes=[smk])
                        S.op("dve", stt_(mm_, mx, 0.125, snk, ALU.mult, ALU.max), reads=[smk, "sinks"], writes=[smk])
                        S.op("dve", ts_(nm, mm_, -1.0, None, ALU.mult), reads=[smk], writes=[smk])
                        S.op("dve", tt_(tmp, snk, nm, ALU.add), reads=[smk, "sinks"], writes=[smk])
                        e_ = E[n % 2]
                        e3 = e_.rearrange("p (q k) -> p q k", q=4)
                        for qh in range(4):
                            S.op("act", act_(e3[:, qh, :], sc3[:, qh, :], AF.Exp, scale=0.125, bias=nm[:, qh:qh + 1]),
                                 reads=pk(sc0 + qh * 256, sc0 + (qh + 1) * 256) + [smk], writes=[("E", n % 2, qh)])
                        S.op("act", act_(es, tmp, AF.Exp), reads=[smk], writes=[("es", n % 2)])
                        p_ = Pm[n % 2]
                        p3 = p_.rearrange("p (q k) -> p q k", q=4)
                        mk = maskA if n == 0 else maskB
                        S.op("pool", tt_(p3, e3, mk[:].unsqueeze(1).broadcast_to([128, 4, 256]), ALU.mult),
                             reads=[("E", n % 2, qh) for qh in range(4)] + ["maskA", "maskB"], writes=[("Pm", n % 2)])
                        ptc = 7 * 512
                        ptp = psb(ptc, ptc + 512)
                        trs = [(ptp[:, (qh * 2 + kt) * 128:(qh * 2 + kt + 1) * 128], p3[:, qh, kt * 128:(kt + 1) * 128], ident_b[:])
                               for qh in range(4) for kt in range(2)]
                        S.op("pe", trgroup(trs), reads=[("Pm", n % 2), "ident_b"], writes=pk(ptc, ptc + 512))
                        pt_ = PTs[n % 2]
                        S.op("act", act_(pt_, ptp, AF.Copy), reads=pk(ptc, ptc + 512), writes=[("PTs", n % 2)])
                        pvc = 2 * 512
                        mms = []
                        for qh in range(4):
                            for kt in range(2):
                                mms.append((psf(pvc + qh * 65, pvc + (qh + 1) * 65), pt_[:, (qh * 2 + kt) * 128:(qh * 2 + kt + 1) * 128],
                                            vS3[:, n + kt, 0:65], kt == 0, kt == 1))
                        S.op("pe", mmgroup(mms), reads=[("PTs", n % 2), "vS", "vS0", "vS1"], writes=pk(pvc, pvc + 260))
                        pv3 = psf(pvc, pvc + 260).rearrange("p (q c) -> p q c", q=4)
                        S.op("dve", tt_(den, pv3[:, :, 64], es, ALU.add), reads=pk(pvc, pvc + 260) + [("es", n % 2)], writes=[("den", n % 2)])
                        S.op("dve", (lambda e, rinv=rinv, den=den: e.reciprocal(out=rinv, in_=den)), reads=[("den", n % 2)], writes=[("rinv", n % 2)])
                        a_ = an[n % 2]
                        S.op("dve", tt_(a_.rearrange("p (q c) -> p q c", q=4), pv3[:, :, 0:64], rinv.unsqueeze(2).broadcast_to([128, 4, 64]), ALU.mult),
                             reads=pk(pvc, pvc + 260) + [("rinv", n % 2)], writes=[("an", n % 2)])
                        atc = (n % 2) * 512
                        atp = psb(atc, atc + 128)
                        S.op("pe", trgroup([(atp[:, 0:128], a_[:, 0:128], ident_b[:]), (atp[:, 128:256], a_[:, 128:256], ident_b[:])]),
                             reads=[("an", n % 2), "ident_b"], writes=pk(atc, atc + 128))
                        for i in range(2):
                            f = h * 2 + i
                            S.op("dve", tt_(brT[:, f, tk], atp[:, i * 128:(i + 1) * 128], sG3[:, i, tk], ALU.mult),
                                 reads=pk(atc, atc + 128) + [("siluG", i, n // 4)], writes=[("brT", f, n // 4)])
            S.barrier()
            cz = Carve()
            PT = cz.bf(KC * T).rearrange("p (k t) -> p k t", k=KC)
            sig = [cz.bf(512), cz.bf(512)]
            s1 = sbase + (48 if branch == 0 else 96)
            if (branch == 0 and DBG < 3) or (branch == 1 and DBG < 5):
                continue
            for c in range(KC):
                w, wk = slab(s1 + c)
                b0 = (c % 2) * 4
                mmsP, mmsM = [], []
                for k in range(KC):
                    for th in range(2):
                        mmsP.append((psf((b0 + th) * 512, (b0 + th + 1) * 512), w[:, k, 0:128], brT[:, k, th * 512:(th + 1) * 512], k == 0, k == KC - 1))
                for k in range(KC):
                    for th in range(2):
                        mmsM.append((psf((b0 + 2 + th) * 512, (b0 + 3 + th) * 512), w[:, k, 128:256], uT[:, k, th * 512:(th + 1) * 512], k == 0, k == KC - 1))
                S.op("pe", mmgroup(mmsP), reads=[wk] + [("brT", k, th) for k in range(KC) for th in range(2)], writes=pk(b0 * 512, (b0 + 2) * 512))
                S.op("pe", mmgroup(mmsM), reads=[wk] + [("uT", k, th) for k in range(KC) for th in range(2)], writes=pk((b0 + 2) * 512, (b0 + 4) * 512))
                for th in range(2):
                    sg_ = sig[th]
                    S.op("act", act_(sg_, psf((b0 + 2 + th) * 512, (b0 + 3 + th) * 512), AF.Sigmoid),
                         reads=pk((b0 + 2 + th) * 512, (b0 + 3 + th) * 512), writes=[("sig", th)])
                    S.op("dve", tt_(PT[:, c, th * 512:(th + 1) * 512], psf((b0 + th) * 512, (b0 + th + 1) * 512), sg_, ALU.mult),
                         reads=pk((b0 + th) * 512, (b0 + th + 1) * 512) + [("sig", th)], writes=[("PT", c, th)])
            s2 = s1 + 16
            for cp in range(8):
                w, wk = slab(s2 + cp)
                for ci in range(2):
                    c = cp * 2 + ci
                    b0 = (c % 2) * 2
                    mms = []
                    for k in range(KC):
                        for th in range(2):
                            mms.append((psf((b0 + th) * 512, (b0 + th + 1) * 512), w[:, k, ci * 128:(ci + 1) * 128], PT[:, k, th * 512:(th + 1) * 512], k == 0, k == KC - 1))
                    S.op("pe", mmgroup(mms), reads=[wk] + [("PT", k, th) for k in range(KC) for th in range(2)], writes=pk(b0 * 512, (b0 + 2) * 512))
                    for th in range(2):
                        xs = xT[:, c, th * 512:(th + 1) * 512]
                        S.op("dve", stt_(xs, psf((b0 + th) * 512, (b0 + th + 1) * 512), gate[:, c:c + 1], xs, ALU.mult, ALU.add),
                             reads=pk((b0 + th) * 512, (b0 + th + 1) * 512) + ["modsb", ("xT", c, th)], writes=[("xT", c, th)])
            S.barrier()

    if final:
        norm_phase(pv[:, PV_FW:PV_FW + 16], None, False)
        S.barrier()
    cz = Carve()
    stage = [cz.f32(2048), cz.f32(2048)]
    for n in range(NT):
        st = stage[n % 2]
        for g4 in range(4):
            bank = (n * 4 + g4) % 2
            c0 = bank * 512
            trs = [(psf(c0 + j * 128, c0 + (j + 1) * 128), xT[:, g4 * 4 + j, n * 128:(n + 1) * 128], ident_f[:]) for j in range(4)]
            S.op("pe", trgroup(trs), reads=[("xT", g4 * 4 + j, n // 4) for j in range(4)] + ["ident_f"], writes=pk(c0, c0 + 512))
            eng = "act" if g4 % 2 == 0 else "dve"
            outv = st[:, g4 * 512:(g4 + 1) * 512]
            inv = psf(c0, c0 + 512)
            fn = act_(outv, inv, AF.Copy) if eng == "act" else cp_(outv, inv)
            S.op(eng, fn, reads=pk(c0, c0 + 512), writes=[("ostage", n % 2, g4)])
        S.dma("sp", (lambda e, n=n, st=st: e.dma_start(out=y_out[n * 128:(n + 1) * 128, :], in_=st)),
              reads=[("ostage", n % 2, g4) for g4 in range(4)], writes=[("y", n)])
    S.barrier()

    semctx = {}
    for n in S.sem_names:
        g = nc.semaphore("s_" + n)
        ctx.append(g)
        semctx[n] = g.__enter__()
    with nc.Block() as block:
        @block.tensor
        def _(e):
            S.emit("pe", e, semctx)

        @block.scalar
        def _(e):
            S.emit("act", e, semctx)

        @block.vector
        def _(e):
            S.emit("dve", e, semctx)

        @block.gpsimd
        def _(e):
            S.emit("pool", e, semctx)

        @block.sync
        def _(e):
            S.emit("sp", e, semctx)
    for g in reversed(ctx):
        g.__exit__(None, None, None)
    return nc


def _slab(w2d):
    return np.ascontiguousarray(w2d.reshape(KC, 128, 256).transpose(1, 0, 2)).reshape(128, 4096)


def build_slabs(l, ada_w, w_in, w_ret_o, w_swa_o, w_out):
    out = np.empty((NSLAB, 128, 4096), np.float32)
    wi = w_in[l]
    i = 0
    for s in range(24):
        out[i] = _slab(ada_w[l][:, s * 256:(s + 1) * 256]); i += 1
    for h in range(8):
        out[i] = _slab(np.concatenate([wi[:, O_RQ + h * 128:O_RQ + (h + 1) * 128], wi[:, O_RK + h * 128:O_RK + (h + 1) * 128]], axis=1)); i += 1
        out[i] = _slab(wi[:, O_RV + h * 256:O_RV + (h + 1) * 256]); i += 1
        out[i] = _slab(wi[:, O_RG + h * 256:O_RG + (h + 1) * 256]); i += 1
    for c in range(16):
        out[i] = _slab(np.concatenate([w_ret_o[l][:, c * 128:(c + 1) * 128], wi[:, O_MR + c * 128:O_MR + (c + 1) * 128]], axis=1)); i += 1
    for cp in range(8):
        out[i] = _slab(w_out[l][:, cp * 256:(cp + 1) * 256]); i += 1
    for h in range(8):
        kk = wi[:, O_SK + h * 64:O_SK + (h + 1) * 64]
        vv = wi[:, O_SV + h * 64:O_SV + (h + 1) * 64]
        out[i] = _slab(np.concatenate([kk, kk, vv, vv], axis=1)); i += 1
        out[i] = _slab(wi[:, O_SQ + h * 256:O_SQ + (h + 1) * 256]); i += 1
        out[i] = _slab(wi[:, O_SG + h * 256:O_SG + (h + 1) * 256]); i += 1
    for c in range(16):
        out[i] = _slab(np.concatenate([w_swa_o[l][:, c * 128:(c + 1) * 128], wi[:, O_MS + c * 128:O_MS + (c + 1) * 128]], axis=1)); i += 1
    for cp in range(8):
        out[i] = _slab(w_out[l][:, cp * 256:(cp + 1) * 256]); i += 1
    assert i == NSLAB
    return out


def const_tables(core):
    j = core % 4
    b = core // 4
    tabs = np.zeros((128, NTAB), np.float64)
    t = np.arange(128, dtype=np.float64)
    inv = 1.0 / (10000.0 ** np.linspace(0.0, 1.0, 64, dtype=np.float32).astype(np.float64))
    for n in range(8):
        pos = (j * 1024 + n * 128 + t)
        ang = (pos[:, None].astype(np.float32) * inv[None, :].astype(np.float32)).astype(np.float64)
        tabs[:, TB_COS + n * 64:TB_COS + (n + 1) * 64] = np.cos(ang)
        tabs[:, TB_SIN + n * 64:TB_SIN + (n + 1) * 64] = np.sin(ang)
    hh = np.arange(8, dtype=np.float64)
    log_g = np.log1p(-np.exp2(-5.0 - hh))
    tabs[:, TB_XI:TB_XI + 8] = np.exp(log_g[None, :] * (t[:, None] + 1.0))
    tabs[:, TB_KZ:TB_KZ + 8] = np.exp(-log_g[None, :] * (t[:, None] + 1.0)) * (128.0 ** -0.5)
    g = np.exp(log_g * 128.0)
    tabs[:, TB_G:TB_G + 8] = g[None, :]
    for n in range(8):
        tabs[:, TB_GPOW + n * 8:TB_GPOW + (n + 1) * 8] = (g ** n)[None, :]
    for i in range(8):
        if i // 4 == b and (i % 4) < j:
            tabs[:, TB_COEF + i * 8:TB_COEF + (i + 1) * 8] = (g ** (8 * (j - 1 - (i % 4))))[None, :]
        tabs[:, TB_SEL + i] = 1.0 if (i == core - 1 and j > 0) else 0.0
    masks = np.zeros((128, 768), np.float32)
    masks[:, 0:128] = np.eye(128, dtype=np.float32)
    mi = np.arange(128)
    masks[:, 128:256] = (mi[:, None] <= mi[None, :]).astype(np.float32)
    prev = (mi[None, :] > mi[:, None]).astype(np.float32)
    cur = (mi[None, :] <= mi[:, None]).astype(np.float32)
    masks[:, 512:640] = prev
    masks[:, 640:768] = cur
    masks[:, 256:384] = prev if j > 0 else 0.0
    masks[:, 384:512] = cur
    return tabs.astype(np.float32), masks


def fm(v):
    return np.ascontiguousarray(v.reshape(KC, 128).T)


_PROG = {}


def run_layers(xs, l0, l1, final, inputs):
    key = (l1 - l0, final)
    if key not in _PROG:
        _PROG[key] = build_program(0, l1 - l0, final)
    nc = _PROG[key]
    c = inputs["c"]
    wsl = np.concatenate([build_slabs(l, inputs["ada_w"], inputs["w_in"], inputs["w_ret_o"], inputs["w_swa_o"], inputs["w_out"])
                          for l in range(l0, l1)], axis=0)
    pvec = np.zeros((128, NPV), np.float32)
    sinks = np.zeros((128, 128), np.float32)
    for li, l in enumerate(range(l0, l1)):
        pvec[:, PV_NW + li * 16:PV_NW + (li + 1) * 16] = fm(inputs["norm_w"][l])
        for s in range(3):
            pvec[:, PV_AB + li * 48 + s * 16:PV_AB + li * 48 + (s + 1) * 16] = fm(inputs["ada_b"][l][s * D:(s + 1) * D])
        pvec[:, PV_GN + li * 16:PV_GN + (li + 1) * 16] = fm(inputs["ret_gn_w"][l])
        sinks[:, li * 32:(li + 1) * 32] = inputs["attn_sinks"][l][None, :]
    pvec[:, PV_FW:PV_FW + 16] = fm(inputs["final_norm_w"])
    in_maps = []
    for core in range(NCORES):
        tabs, masks = const_tables(core)
        in_maps.append({"x": xs[core], "wslab": wsl, "cvec": fm(c[core // 4]), "pvec": pvec, "sinks": sinks,
                        "tabs": tabs, "masks": masks})
    res = run_bass_kernel_spmd(nc, in_maps, core_ids=list(range(NCORES)))
    return [np.asarray(r["y"]) for r in res.results]


FUSED = True


def kernel(x, c, norm_w, ada_w, ada_b, w_in, ret_gn_w, attn_sinks, w_ret_o, w_swa_o, w_out, final_norm_w):
    inputs = dict(c=np.asarray(c, np.float32), norm_w=np.asarray(norm_w, np.float32), ada_w=np.asarray(ada_w, np.float32),
                  ada_b=np.asarray(ada_b, np.float32), w_in=np.asarray(w_in, np.float32), ret_gn_w=np.asarray(ret_gn_w, np.float32),
                  attn_sinks=np.asarray(attn_sinks, np.float32), w_ret_o=np.asarray(w_ret_o, np.float32),
                  w_swa_o=np.asarray(w_swa_o, np.float32), w_out=np.asarray(w_out, np.float32),
                  final_norm_w=np.asarray(final_norm_w, np.float32))
    x = np.asarray(x, np.float32)
    xs = [np.ascontiguousarray(x[core // 4, (core % 4) * T:(core % 4 + 1) * T, :]) for core in range(NCORES)]
    if FUSED:
        xs = run_layers(xs, 0, DEPTH, True, inputs)
    else:
        for l in range(DEPTH):
            xs = run_layers(xs, l, l + 1, l == DEPTH - 1, inputs)
    out = np.empty((2, 4 * T, D), np.float32)
    for core in range(NCORES):
        out[core // 4, (core % 4) * T:(core % 4 + 1) * T, :] = xs[core]
    return out
```

```python
import numpy as np
import concourse.bass as bass
import concourse.mybir as mybir
from concourse.bass_utils import run_bass_kernel_spmd

F32 = mybir.dt.float32
BF16 = mybir.dt.bfloat16
AF = mybir.ActivationFunctionType
ALU = mybir.AluOpType
AX = mybir.AxisListType

D = 2048
KC = 16
T = 1024
NT = 8
DEPTH = 4
NCORES = 8
EPS = 1e-6
NSLAB = 120
RING = 3
import os
DBG = int(os.environ.get('KDBG', '9'))
O_RQ, O_RK, O_RV, O_RG, O_SQ, O_SK, O_SV, O_SG, O_MR, O_MS = (
    0, 1024, 2048, 4096, 6144, 8192, 8704, 9216, 11264, 13312)

TB_COS, TB_SIN, TB_XI, TB_KZ, TB_G, TB_GPOW, TB_COEF, TB_SEL = 0, 512, 1024, 1032, 1040, 1048, 1112, 1176
NTAB = 1184
PV_NW, PV_AB, PV_GN, PV_FW = 0, 64, 256, 320
NPV = 336


SELF_SKIP = set(os.environ.get('KSELF', 'pe').split(','))


class Sched:
    ENGS = ("pe", "act", "dve", "pool", "sp")

    def __init__(self, ndma_w=8, ndma_m=8):
        self.q = {e: [] for e in self.ENGS}
        self.cnt = {e: 0 for e in self.ENGS}
        self.waited = {e: {} for e in self.ENGS}
        self.lastw = {}
        self.readers = {}
        self.dsem = {"w": ["dw%d" % i for i in range(ndma_w)], "m": ["dm%d" % i for i in range(ndma_m)]}
        self.dcnt = {n: 0 for k in self.dsem for n in self.dsem[k]}
        self.dcnt["cc"] = 0
        self.drr = {"w": 0, "m": 0}
        self.sem_names = list(self.ENGS) + list(self.dcnt.keys())

    def _deps(self, reads, writes):
        toks = []
        for k in reads:
            if k in self.lastw:
                toks.append(self.lastw[k])
        for k in writes:
            if k in self.lastw:
                toks.append(self.lastw[k])
            r = self.readers.get(k)
            if r:
                toks.extend(r.items())
        return toks

    def _waits(self, eng, toks):
        need = {}
        for sem, val in toks:
            if val > need.get(sem, 0):
                need[sem] = val
        w = self.waited[eng]
        for sem, val in need.items():
            if eng == sem and eng in SELF_SKIP:
                continue
            if w.get(sem, 0) >= val:
                continue
            w[sem] = val
            self.q[eng].append(("wait", sem, val))

    def _record(self, tok, reads, writes):
        for k in reads:
            r = self.readers.setdefault(k, {})
            if tok[1] > r.get(tok[0], 0):
                r[tok[0]] = tok[1]
        for k in writes:
            self.lastw[k] = tok
            self.readers[k] = {}

    def op(self, eng, fn, reads=(), writes=()):
        self._waits(eng, self._deps(reads, writes))
        self.cnt[eng] += 1
        tok = (eng, self.cnt[eng])
        self.q[eng].append(("op", fn, eng, 1))
        self._record(tok, reads, writes)
        return tok

    def dma(self, issuer, fn, reads=(), writes=(), kind="m"):
        names = self.dsem[kind]
        ds = names[self.drr[kind] % len(names)]
        self.drr[kind] += 1
        toks = self._deps(reads, writes)
        if self.dcnt[ds] > 0:
            toks.append((ds, self.dcnt[ds]))
        self._waits(issuer, toks)
        self.dcnt[ds] += 16
        tok = (ds, self.dcnt[ds])
        self.q[issuer].append(("op", fn, ds, 16))
        self._record(tok, reads, writes)
        return tok

    def collective(self, fn, reads=(), writes=()):
        toks = self._deps(reads, writes)
        self._waits("pool", toks)
        self.dcnt["cc"] += 1
        tok = ("cc", self.dcnt["cc"])
        self.q["pool"].append(("op", fn, "cc", 1))
        self._record(tok, reads, writes)
        return tok

    def barrier(self):
        toks = [(e, self.cnt[e]) for e in self.ENGS if self.cnt[e] > 0]
        toks += [(n, v) for n, v in self.dcnt.items() if v > 0]
        for e in self.ENGS:
            self._waits(e, [t for t in toks if t[0] != e or e != "pe"])

    def emit(self, eng, handle, sems):
        for it in self.q[eng]:
            if it[0] == "wait":
                handle.wait_ge(sems[it[1]], it[2])
            else:
                inst = it[1](handle)
                inst.then_inc(sems[it[2]], it[3])


def pk(c0, c1):
    return [("ps", q) for q in range(c0 // 512, (c1 + 511) // 512)]


def build_program(l0, l1, final):
    nl = l1 - l0
    nc = bass.Bass("TRN2", target_bir_lowering=False)
    x_in = nc.dram_tensor("x", [T, D], F32, kind="ExternalInput").ap()
    wsl = nc.dram_tensor("wslab", [nl * NSLAB, 128, 4096], F32, kind="ExternalInput").ap()
    cvec = nc.dram_tensor("cvec", [128, KC], F32, kind="ExternalInput").ap()
    pvec = nc.dram_tensor("pvec", [128, NPV], F32, kind="ExternalInput").ap()
    sinks_in = nc.dram_tensor("sinks", [128, 128], F32, kind="ExternalInput").ap()
    tabs_in = nc.dram_tensor("tabs", [128, NTAB], F32, kind="ExternalInput").ap()
    masks_in = nc.dram_tensor("masks", [128, 768], F32, kind="ExternalInput").ap()
    y_out = nc.dram_tensor("y", [T, D], F32, kind="ExternalOutput").ap()
    g_in = [[nc.dram_tensor("gin_%d_%d" % (l, h), [128, 256], F32) for h in range(8)] for l in range(nl)]
    g_out = [[nc.dram_tensor("gout_%d_%d" % (l, h), [NCORES * 128, 256], F32) for h in range(8)] for l in range(nl)]
    h_in = [[nc.dram_tensor("hin_%d_%d" % (l, h), [128, 192], BF16) for h in range(8)] for l in range(nl)]
    h_out = [[nc.dram_tensor("hout_%d_%d" % (l, h), [NCORES * 128, 192], BF16) for h in range(8)] for l in range(nl)]

    S = Sched()
    ctx = []

    def sb(name, shape, dt):
        g = nc.sbuf_tensor("sb_" + name, shape, dt)
        ctx.append(g)
        return g.__enter__()

    xT = sb("xT", [128, KC, T], F32)
    uT = sb("uT", [128, KC, T], BF16)
    brT = sb("brT", [128, KC, T], BF16)
    ring = sb("ring", [128, RING, 4096], BF16)
    PRN = 22144
    PR = sb("PR", [128, PRN], BF16)
    tabs = sb("tabs", [128, NTAB], F32)
    pv = sb("pv", [128, NPV], F32)
    sinks = sb("sinks", [128, 128], F32)
    cv = sb("cv", [128, KC], F32)
    cact = sb("cact", [128, KC], BF16)
    ident_f = sb("ident_f", [128, 128], F32)
    ident_b = sb("ident_b", [128, 128], BF16)
    ones_b = sb("ones_b", [128, 128], BF16)
    causal_b = sb("causal_b", [128, 128], BF16)
    maskA = sb("maskA", [128, 256], BF16)
    maskB = sb("maskB", [128, 256], BF16)
    modsb = sb("modsb", [128, 48], F32)
    asb = sb("asb", [128, KC], F32)
    small = sb("small", [128, 96], F32)
    pg = nc.psum_tensor("pbig", [128, 4096], F32)
    ctx.append(pg)
    PB = pg.__enter__()
    PBh = PB[:, :].bitcast(BF16)

    def psf(c0, c1):
        return PB[:, c0:c1]

    def psb(c0, c1):
        return PBh[:, 2 * c0:2 * c1]

    class Carve:
        def __init__(self):
            self.off = 0

        def bf(self, n):
            a = PR[:, self.off:self.off + n]
            self.off += n
            assert self.off <= PRN, self.off
            return a

        def f32(self, n):
            a = PR[:, self.off:self.off + 2 * n].bitcast(F32)
            self.off += 2 * n
            assert self.off <= PRN, self.off
            return a

    act_ = lambda out, in_, func, **kw: (lambda e: e.activation(out=out, in_=in_, func=func, **kw))
    tt_ = lambda out, a, b, op: (lambda e: e.tensor_tensor(out=out, in0=a, in1=b, op=op))
    ts_ = lambda out, a, s1, s2, op0, op1=None: (
        (lambda e: e.tensor_scalar(out=out, in0=a, scalar1=s1, scalar2=s2, op0=op0, op1=op1)) if op1 is not None
        else (lambda e: e.tensor_scalar(out=out, in0=a, scalar1=s1, scalar2=None, op0=op0)))
    stt_ = lambda out, a, s, b, op0, op1: (lambda e: e.scalar_tensor_tensor(out=out, in0=a, scalar=s, in1=b, op0=op0, op1=op1))
    cp_ = lambda out, in_: (lambda e: e.tensor_copy(out=out, in_=in_))

    def mmgroup(mms):
        def f(pe):
            inst = None
            for (o, l, r, st, sp) in mms:
                inst = pe.matmul(o, lhsT=l, rhs=r, start=st, stop=sp)
            return inst
        return f

    def trgroup(trs):
        def f(pe):
            inst = None
            for (o, i, idn) in trs:
                inst = pe.transpose(o, i, idn)
            return inst
        return f

    wstate = {"issued": 0}
    total_slabs = nl * NSLAB

    def slab_issue_upto(n):
        n = min(n, total_slabs - 1)
        while wstate["issued"] <= n:
            i = wstate["issued"]
            slot = i % RING
            S.dma("pool", (lambda e, i=i, slot=slot: e.dma_start(out=ring[:, slot, :], in_=wsl[i], max_dma_last_dim=8192)),
                  writes=[("ring", slot)], kind="w")
            wstate["issued"] += 1

    def slab(i):
        slab_issue_upto(i + RING - 2)
        slot = i % RING
        return ring[:, slot, :].rearrange("p (k c) -> p k c", k=KC), ("ring", slot)

    S.dma("sp", lambda e: e.dma_start(out=tabs[:], in_=tabs_in), writes=["tabs"])
    S.dma("sp", lambda e: e.dma_start(out=pv[:], in_=pvec), writes=["pv"])
    S.dma("sp", lambda e: e.dma_start(out=sinks[:], in_=sinks_in), writes=["sinks"])
    S.dma("sp", lambda e: e.dma_start(out=cv[:], in_=cvec), writes=["cv"])
    cz = Carve()
    mstage = cz.f32(768)
    stage = [cz.f32(2048), cz.f32(2048)]
    S.dma("sp", lambda e: e.dma_start(out=mstage, in_=masks_in), writes=["mstage"])
    S.op("dve", cp_(ident_f[:], mstage[:, 0:128]), reads=["mstage"], writes=["ident_f"])
    S.op("dve", cp_(ident_b[:], mstage[:, 0:128]), reads=["mstage"], writes=["ident_b"])
    S.op("dve", cp_(causal_b[:], mstage[:, 128:256]), reads=["mstage"], writes=["causal_b"])
    S.op("dve", cp_(maskA[:], mstage[:, 256:512]), reads=["mstage"], writes=["maskA"])
    S.op("dve", cp_(maskB[:], mstage[:, 512:768]), reads=["mstage"], writes=["maskB"])
    S.op("pool", lambda e: e.memset(ones_b[:], 1.0), writes=["ones_b"])
    S.op("pool", lambda e: e.memset(small[:, 0:1], EPS), writes=["epsc"])
    epsc = small[:, 0:1]
    S.op("act", act_(cact[:], cv[:], AF.Silu), reads=["cv"], writes=["cact"])
    slab_issue_upto(RING - 1)

    for n in range(NT):
        st = stage[n % 2]
        S.dma("sp", (lambda e, n=n, st=st: e.dma_start(out=st, in_=x_in[n * 128:(n + 1) * 128, :])),
              writes=[("stage", n % 2)])
        for g4 in range(4):
            bank = (n * 4 + g4) % 2
            c0 = bank * 512
            trs = [(psf(c0 + j * 128, c0 + (j + 1) * 128), st[:, (g4 * 4 + j) * 128:(g4 * 4 + j + 1) * 128], ident_f[:])
                   for j in range(4)]
            S.op("pe", trgroup(trs), reads=[("stage", n % 2), "ident_f"], writes=pk(c0, c0 + 512))
            eng = "act" if g4 % 2 == 0 else "dve"
            outv = xT[:, g4 * 4:(g4 + 1) * 4, n * 128:(n + 1) * 128]
            inv = psf(c0, c0 + 512).rearrange("p (j t) -> p j t", j=4)
            fn = (lambda e, o=outv, i=inv: e.activation(out=o, in_=i, func=AF.Copy)) if eng == "act" else cp_(outv, inv)
            S.op(eng, fn, reads=pk(c0, c0 + 512), writes=[("xT", k, n // 4) for k in range(g4 * 4, g4 * 4 + 4)])
    S.barrier()

    def norm_phase(wcol_ap, with_shift, dst_is_u):
        cz = Carve()
        rstd = cz.f32(1024)
        sq = [cz.bf(512), cz.bf(512)]
        xn = [cz.f32(512), cz.f32(512)]
        for th in range(2):
            c0 = (3 + th) * 512
            for k in range(KC):
                s_ = sq[k % 2]
                S.op("act", act_(s_, xT[:, k, th * 512:(th + 1) * 512], AF.Square), reads=[("xT", k, th)], writes=[("sq", k % 2)])
                S.op("pe", mmgroup([(psf(c0, c0 + 512), ones_b[:], s_, k == 0, k == KC - 1)]),
                     reads=[("sq", k % 2), "ones_b"], writes=pk(c0, c0 + 512))
            rs = rstd[:, th * 512:(th + 1) * 512]
            S.op("act", act_(rs, psf(c0, c0 + 512), AF.Sqrt, scale=1.0 / D, bias=epsc), reads=pk(c0, c0 + 512) + ["epsc"], writes=[("rstd", th)])
            S.op("dve", (lambda e, rs=rs: e.reciprocal(out=rs, in_=rs)), reads=[("rstd", th)], writes=[("rstd", th)])
        i = 0
        for th in range(2):
            for k in range(KC):
                rs = rstd[:, th * 512:(th + 1) * 512]
                xs = xT[:, k, th * 512:(th + 1) * 512]
                t_ = xn[i % 2]
                eng = "dve" if i % 2 == 0 else "pool"
                S.op(eng, tt_(t_, xs, rs, ALU.mult), reads=[("xT", k, th), ("rstd", th)], writes=[("xn", i % 2)])
                if dst_is_u:
                    S.op("act", act_(uT[:, k, th * 512:(th + 1) * 512], t_, AF.Identity, scale=asb[:, k:k + 1], bias=with_shift[:, k:k + 1]),
                         reads=[("xn", i % 2), "asb", "modsb"], writes=[("uT", k, th)])
                else:
                    S.op("act", act_(xs, t_, AF.Identity, scale=wcol_ap[:, k:k + 1]), reads=[("xn", i % 2), "pv"], writes=[("xT", k, th)])
                i += 1

    for li in range(nl):
        l = l0 + li
        sbase = li * NSLAB
        mc0 = 2 * 512
        for i in (range(24) if DBG >= 1 else []):
            w, wk = slab(sbase + i)
            mms = []
            for cc in range(2):
                col = mc0 + i * 2 + cc
                for k in range(KC):
                    mms.append((psf(col, col + 1), w[:, k, cc * 128:(cc + 1) * 128], cact[:, k:k + 1], k == 0, k == KC - 1))
            S.op("pe", mmgroup(mms), reads=[wk, "cact"], writes=pk(mc0, mc0 + 48))
        S.op("dve", tt_(modsb[:], psf(mc0, mc0 + 48), pv[:, PV_AB + li * 48:PV_AB + (li + 1) * 48], ALU.add),
             reads=pk(mc0, mc0 + 48) + ["pv"], writes=["modsb"])
        S.op("dve", stt_(asb[:], modsb[:, 16:32], 1.0, pv[:, PV_NW + li * 16:PV_NW + (li + 1) * 16], ALU.add, ALU.mult),
             reads=["modsb", "pv"], writes=["asb"])
        shift = modsb[:, 0:16]
        gate = modsb[:, 32:48]
        if DBG >= 1:
            norm_phase(None, shift, True)
        S.barrier()

        for branch in range(2):
            if (branch == 0 and DBG < 2) or (branch == 1 and DBG < 4):
                continue
            if branch == 0:
                cz = Carve()
                qkf = [cz.f32(256), cz.f32(256)]
                t1 = cz.f32(256); t2 = cz.f32(256); ssum = cz.f32(256)
                qhat = [cz.bf(128), cz.bf(128)]
                khat = cz.bf(1024)
                vtok2 = [cz.bf(2048), cz.bf(2048)]
                qkT2 = [cz.bf(2048), cz.bf(2048)]
                siluG = cz.bf(2048)
                Sloc = cz.f32(2048)
                Sbf = [cz.bf(256), cz.bf(256)]
                gath = [cz.f32(256), cz.f32(256)]
                Sin = cz.f32(256)
                Gn = cz.f32(256)
                lend = cz.f32(256)
                Pm = [cz.bf(128), cz.bf(128)]
                rhat = [cz.bf(256), cz.bf(256)]
                st6 = [cz.f32(6), cz.f32(6)]
                mv = [cz.f32(2), cz.f32(2)]
                rsr = [cz.f32(2), cz.f32(2)]
                sG3 = siluG.rearrange("p (a t) -> p a t", a=2)
                slabs_a = {}

                def hv(h):
                    return (tabs[:, TB_XI + h:TB_XI + h + 1], tabs[:, TB_KZ + h:TB_KZ + h + 1], tabs[:, TB_G + h:TB_G + h + 1])

                def a_proj(h, n):
                    if (h, "qk") not in slabs_a:
                        slabs_a[(h, "qk")] = slab(sbase + 24 + 3 * h)
                        slabs_a[(h, "v")] = slab(sbase + 24 + 3 * h + 1)
                    wqk, kqk = slabs_a[(h, "qk")]
                    wv, kv_ = slabs_a[(h, "v")]
                    xi, kz, gh = hv(h)
                    vtok = vtok2[h % 2]
                    tk = slice(n * 128, (n + 1) * 128)
                    th = n // 4
                    c0 = (n % 2) * 512
                    mms = [(psf(c0, c0 + 256), uT[:, k, tk], wqk[:, k, :], k == 0, k == KC - 1) for k in range(KC)]
                    mms += [(psf(c0 + 256, c0 + 512), uT[:, k, tk], wv[:, k, :], k == 0, k == KC - 1) for k in range(KC)]
                    S.op("pe", mmgroup(mms), reads=[kqk, kv_] + [("uT", k, th) for k in range(KC)], writes=pk(c0, c0 + 512))
                    qf = qkf[n % 2]
                    S.op("act", act_(qf, psf(c0, c0 + 256), AF.Copy), reads=pk(c0, c0 + 256), writes=[("qkf", n % 2)])
                    S.op("act", act_(vtok[:, n * 256:(n + 1) * 256], psf(c0 + 256, c0 + 512), AF.Copy),
                         reads=pk(c0 + 256, c0 + 512), writes=[("vtok", h % 2, n)])
                    q4 = qf.rearrange("p (a b d) -> p a b d", a=2, b=2)
                    t14 = t1.rearrange("p (a b d) -> p a b d", a=2, b=2)
                    t24 = t2.rearrange("p (a b d) -> p a b d", a=2, b=2)
                    s4 = ssum.rearrange("p (a b d) -> p a b d", a=2, b=2)
                    cosn = tabs[:, TB_COS + n * 64:TB_COS + (n + 1) * 64]
                    sinn = tabs[:, TB_SIN + n * 64:TB_SIN + (n + 1) * 64]
                    cos4 = cosn.unsqueeze(1).unsqueeze(1).broadcast_to([128, 2, 2, 64])
                    sin3 = sinn.unsqueeze(1).broadcast_to([128, 2, 64])
                    S.op("pool", tt_(t14, q4, cos4, ALU.mult), reads=[("qkf", n % 2), "tabs"], writes=["t1"])
                    S.op("pool", tt_(t24[:, :, 0, :], q4[:, :, 1, :], sin3, ALU.mult), reads=[("qkf", n % 2), "tabs"], writes=["t2a"])
                    S.op("pool", tt_(t24[:, :, 1, :], q4[:, :, 0, :], sin3, ALU.mult), reads=[("qkf", n % 2), "tabs"], writes=["t2b"])
                    S.op("pool", tt_(s4[:, :, 0, :], t14[:, :, 0, :], t24[:, :, 0, :], ALU.subtract), reads=["t1", "t2a"], writes=["ssa"])
                    S.op("pool", tt_(s4[:, :, 1, :], t14[:, :, 1, :], t24[:, :, 1, :], ALU.add), reads=["t1", "t2b"], writes=["ssb"])
                    S.op("act", act_(qhat[n % 2], ssum[:, 0:128], AF.Identity, scale=xi), reads=["ssa", "ssb", "tabs"], writes=[("qhat", n % 2)])
                    S.op("act", act_(khat[:, tk], ssum[:, 128:256], AF.Identity, scale=kz), reads=["ssa", "ssb", "tabs"], writes=[("khat", n)])

                def a_tr(h, n):
                    tk = slice(n * 128, (n + 1) * 128)
                    qkT3 = qkT2[h % 2].rearrange("p (a t) -> p a t", a=2)
                    tb = (4 + n % 2) * 512
                    tp = psb(tb, tb + 128)
                    S.op("pe", trgroup([(tp[:, 0:128], qhat[n % 2], ident_b[:]), (tp[:, 128:256], khat[:, tk], ident_b[:])]),
                         reads=[("qhat", n % 2), ("khat", n), "ident_b"], writes=pk(tb, tb + 128))
                    S.op("dve", cp_(qkT3[:, :, tk], tp.rearrange("p (a t) -> p a t", a=2)), reads=pk(tb, tb + 128), writes=[("qkT", h % 2, n)])

                def a_chain_send(h):
                    xi, kz, gh = hv(h)
                    vtok = vtok2[h % 2]
                    for n in range(NT):
                        tk = slice(n * 128, (n + 1) * 128)
                        kc0 = (6 + n % 2) * 512
                        S.op("pe", mmgroup([(psf(kc0, kc0 + 256), khat[:, tk], vtok[:, n * 256:(n + 1) * 256], True, True)]),
                             reads=[("khat", n), ("vtok", h % 2, n)], writes=pk(kc0, kc0 + 256))
                        if n == 0:
                            S.op("dve", cp_(Sloc[:, 0:256], psf(kc0, kc0 + 256)), reads=pk(kc0, kc0 + 256), writes=[("Sloc", 0)])
                        else:
                            S.op("dve", stt_(Sloc[:, n * 256:(n + 1) * 256], Sloc[:, (n - 1) * 256:n * 256], gh, psf(kc0, kc0 + 256), ALU.mult, ALU.add),
                                 reads=pk(kc0, kc0 + 256) + [("Sloc", n - 1), "tabs"], writes=[("Sloc", n)])
                    S.op("act", act_(lend, Sloc[:, 7 * 256:8 * 256], AF.Identity, scale=gh), reads=[("Sloc", 7), "tabs"], writes=["lend"])
                    gi = g_in[li][h]
                    go = g_out[li][h]
                    S.dma("sp", (lambda e, gi=gi: e.dma_start(out=gi.ap(), in_=lend)), reads=["lend"], writes=[("gin", li, h)])
                    S.collective((lambda e, gi=gi, go=go: e.collective_compute(
                        "AllGather", ALU.bypass, replica_groups=[list(range(NCORES))], ins=[gi.ap().opt()], outs=[go.ap().opt()])),
                        reads=[("gin", li, h)], writes=[("gout", li, h)])

                def a_gate(h, fc, th):
                    if (h, "g") not in slabs_a:
                        slabs_a[(h, "g")] = slab(sbase + 24 + 3 * h + 2)
                    wg, kg = slabs_a[(h, "g")]
                    gc0 = (2 + (fc * 2 + th) % 2) * 512
                    mms = [(psf(gc0, gc0 + 512), wg[:, k, fc * 128:(fc + 1) * 128], uT[:, k, th * 512:(th + 1) * 512], k == 0, k == KC - 1)
                           for k in range(KC)]
                    S.op("pe", mmgroup(mms), reads=[kg] + [("uT", k, th) for k in range(KC)], writes=pk(gc0, gc0 + 512))
                    S.op("act", act_(sG3[:, fc, th * 512:(th + 1) * 512], psf(gc0, gc0 + 512), AF.Silu),
                         reads=pk(gc0, gc0 + 512), writes=[("siluG", fc, th)])

                def a_sin(h):
                    go = g_out[li][h]
                    for i in range(NCORES):
                        gb = gath[i % 2]
                        S.dma("sp", (lambda e, go=go, gb=gb, i=i: e.dma_start(out=gb, in_=go.ap()[i * 128:(i + 1) * 128, :])),
                              reads=[("gout", li, h)], writes=[("gath", i % 2)])
                        cf = tabs[:, TB_COEF + i * 8 + h:TB_COEF + i * 8 + h + 1]
                        if i == 0:
                            S.op("dve", ts_(Sin, gb, cf, None, ALU.mult), reads=[("gath", 0), "tabs"], writes=["Sin"])
                        else:
                            S.op("dve", stt_(Sin, gb, cf, Sin, ALU.mult, ALU.add), reads=[("gath", i % 2), "tabs", "Sin"], writes=["Sin"])

                def b1(h, n):
                    xi, kz, gh = hv(h)
                    qkT3 = qkT2[h % 2].rearrange("p (a t) -> p a t", a=2)
                    tk = slice(n * 128, (n + 1) * 128)
                    sb_ = Sbf[n % 2]
                    if n == 0:
                        S.op("act", act_(sb_, Sin, AF.Copy), reads=["Sin"], writes=[("Sbf", 0)])
                    else:
                        gp = tabs[:, TB_GPOW + n * 8 + h:TB_GPOW + n * 8 + h + 1]
                        S.op("act", act_(Gn, Sin, AF.Identity, scale=gp), reads=["Sin", "tabs"], writes=["Gn"])
                        S.op("dve", stt_(sb_, Sloc[:, (n - 1) * 256:n * 256], gh, Gn, ALU.mult, ALU.add),
                             reads=[("Sloc", n - 1), "Gn", "tabs"], writes=[("Sbf", n % 2)])
                    sc0 = (6 + n % 2) * 512 + 256
                    S.op("pe", mmgroup([(psf(sc0, sc0 + 128), qkT3[:, 1, tk], qkT3[:, 0, tk], True, True)]),
                         reads=[("qkT", h % 2, n)], writes=pk(sc0, sc0 + 128))
                    S.op("dve", tt_(Pm[n % 2], psf(sc0, sc0 + 128), causal_b[:], ALU.mult), reads=pk(sc0, sc0 + 128) + ["causal_b"], writes=[("Pm", n % 2)])

                def b2(h, n):
                    qkT3 = qkT2[h % 2].rearrange("p (a t) -> p a t", a=2)
                    vtok = vtok2[h % 2]
                    tk = slice(n * 128, (n + 1) * 128)
                    sb_ = Sbf[n % 2]
                    pm_ = Pm[n % 2]
                    oc0 = (6 + (n + 1) % 2) * 512
                    S.op("pe", mmgroup([(psf(oc0, oc0 + 256), pm_, vtok[:, n * 256:(n + 1) * 256], True, False),
                                        (psf(oc0, oc0 + 256), qkT3[:, 0, tk], sb_, False, True)]),
                         reads=[("Pm", n % 2), ("vtok", h % 2, n), ("qkT", h % 2, n), ("Sbf", n % 2)], writes=pk(oc0, oc0 + 256))
                    s6 = st6[n % 2]; m2 = mv[n % 2]; r1 = rsr[n % 2]; rh = rhat[n % 2]
                    S.op("dve", (lambda e, s6=s6, oc0=oc0: e.bn_stats(out=s6, in_=psf(oc0, oc0 + 256))), reads=pk(oc0, oc0 + 256), writes=[("st6", n % 2)])
                    S.op("dve", (lambda e, s6=s6, m2=m2: e.bn_aggr(out=m2, in_=s6)), reads=[("st6", n % 2)], writes=[("mv", n % 2)])
                    S.op("act", act_(r1[:, 0:1], m2[:, 1:2], AF.Sqrt, scale=1.0, bias=epsc), reads=[("mv", n % 2), "epsc"], writes=[("rsr", n % 2)])
                    S.op("dve", (lambda e, r1=r1: e.reciprocal(out=r1[:, 0:1], in_=r1[:, 0:1])), reads=[("rsr", n % 2)], writes=[("rsr", n % 2)])
                    S.op("dve", stt_(r1[:, 1:2], m2[:, 0:1], -1.0, r1[:, 0:1], ALU.mult, ALU.mult), reads=[("mv", n % 2), ("rsr", n % 2)], writes=[("nmr", n % 2)])
                    S.op("act", act_(rh, psf(oc0, oc0 + 256), AF.Identity, scale=r1[:, 0:1], bias=r1[:, 1:2]),
                         reads=pk(oc0, oc0 + 256) + [("nmr", n % 2), ("rsr", n % 2)], writes=[("rhat", n % 2)])

                def b3(h, n):
                    tk = slice(n * 128, (n + 1) * 128)
                    rh = rhat[n % 2]
                    rb = (2 + n % 2) * 512
                    rp = psb(rb, rb + 128)
                    S.op("pe", trgroup([(rp[:, 0:128], rh[:, 0:128], ident_b[:]), (rp[:, 128:256], rh[:, 128:256], ident_b[:])]),
                         reads=[("rhat", n % 2), "ident_b"], writes=pk(rb, rb + 128))
                    for i in range(2):
                        f = h * 2 + i
                        gw = pv[:, PV_GN + li * 16 + f:PV_GN + li * 16 + f + 1]
                        S.op("dve", stt_(brT[:, f, tk], rp[:, i * 128:(i + 1) * 128], gw, sG3[:, i, tk], ALU.mult, ALU.mult),
                             reads=pk(rb, rb + 128) + [("siluG", i, n // 4), "pv"], writes=[("brT", f, n // 4)])

                for t_ in range(NT + 1):
                    if t_ < NT:
                        a_proj(0, t_)
                    if t_ >= 1:
                        a_tr(0, t_ - 1)
                a_chain_send(0)
                for fc in range(2):
                    for th in range(2):
                        a_gate(0, fc, th)
                for h in range(8):
                    a_sin(h)
                    for t_ in range(13):
                        if 0 <= t_ - 2 < NT:
                            b1(h, t_ - 2)
                        if 0 <= t_ - 3 < NT:
                            b2(h, t_ - 3)
                        if 0 <= t_ - 4 < NT:
                            b3(h, t_ - 4)
                        if h < 7:
                            if t_ < NT:
                                a_proj(h + 1, t_)
                            if 1 <= t_ <= NT:
                                a_tr(h + 1, t_ - 1)
                            if t_ == 8:
                                a_gate(h + 1, 0, 0)
                            if t_ == 9:
                                a_gate(h + 1, 1, 0)
                            if t_ == 10:
                                a_chain_send(h + 1)
                            if t_ == 11:
                                a_gate(h + 1, 0, 1)
                            if t_ == 12:
                                a_gate(h + 1, 1, 1)
            else:
                cz = Carve()
                kTs = cz.bf(1152)
                vS = cz.bf(9 * 66)
                qTs = cz.bf(4096)
                siluG = cz.bf(2048)
                hg = cz.bf(8 * 192)
                hacc = cz.f32(192)
                E = [cz.bf(1024), cz.bf(1024)]
                Pm = [cz.bf(1024), cz.bf(1024)]
                PTs = [cz.bf(1024), cz.bf(1024)]
                an = [cz.bf(256), cz.bf(256)]
                sm = [cz.f32(32), cz.f32(32), cz.f32(32), cz.f32(32)]
                vS3 = vS.rearrange("p (b c) -> p b c", c=66)
                qT3 = qTs.rearrange("p (a t) -> p a t", a=4)
                sG3 = siluG.rearrange("p (a t) -> p a t", a=2)
                hg3 = hg.rearrange("p (r c) -> p r c", c=192)
                S.op("pool", lambda e: e.memset(vS3[:, :, 64:66], 1.0), writes=["vS1"])
                S.op("pool", lambda e: e.memset(qTs, 0.0), writes=[("qTs", fc, th) for fc in range(2) for th in range(2)])
                for h in range(8):
                    wkv, kkv = slab(sbase + 72 + 3 * h)
                    for th in range(2):
                        c0 = (th % 2) * 512
                        mms = [(psf(c0, c0 + 512), wkv[:, k, 0:128], uT[:, k, th * 512:(th + 1) * 512], k == 0, k == KC - 1) for k in range(KC)]
                        S.op("pe", mmgroup(mms), reads=[kkv] + [("uT", k, th) for k in range(KC)], writes=pk(c0, c0 + 512))
                        S.op("act", act_(kTs[:, 128 + th * 512:128 + (th + 1) * 512], psf(c0, c0 + 512), AF.Copy),
                             reads=pk(c0, c0 + 512), writes=[("kTs", 1 + th)])
                    vc0 = 2 * 512
                    mms = []
                    for n in range(NT):
                        mms += [(psf(vc0 + n * 64, vc0 + (n + 1) * 64), uT[:, k, n * 128:(n + 1) * 128], wkv[:, k, 128:192], k == 0, k == KC - 1)
                                for k in range(KC)]
                    S.op("pe", mmgroup(mms), reads=[kkv] + [("uT", k, th) for k in range(KC) for th in range(2)], writes=pk(vc0, vc0 + 512))
                    S.op("dve", cp_(vS3[:, 1:9, 0:64], psf(vc0, vc0 + 512).rearrange("p (b c) -> p b c", c=64)),
                         reads=pk(vc0, vc0 + 512), writes=["vS"])
                    hi = h_in[li][h]
                    ho = h_out[li][h]
                    S.dma("sp", (lambda e, hi=hi: e.dma_start(out=hi.ap()[:, 0:128], in_=kTs[:, 1024:1152])), reads=[("kTs", 2)], writes=[("hin", li, h, 0)])
                    S.dma("sp", (lambda e, hi=hi: e.dma_start(out=hi.ap()[:, 128:192], in_=vS3[:, 8, 0:64])), reads=["vS"], writes=[("hin", li, h, 1)])
                    S.collective((lambda e, hi=hi, ho=ho: e.collective_compute(
                        "AllGather", ALU.bypass, replica_groups=[list(range(NCORES))], ins=[hi.ap().opt()], outs=[ho.ap().opt()])),
                        reads=[("hin", li, h, 0), ("hin", li, h, 1)], writes=[("hout", li, h)])
                    for is_gate in (False, True):
                        wsrc, wkey = slab(sbase + 72 + 3 * h + (2 if is_gate else 1))
                        for fc in range(2):
                            for th in range(2):
                                c0 = ((fc * 2 + th) % 2) * 512
                                mms = [(psf(c0, c0 + 512), wsrc[:, k, fc * 128:(fc + 1) * 128], uT[:, k, th * 512:(th + 1) * 512], k == 0, k == KC - 1)
                                       for k in range(KC)]
                                S.op("pe", mmgroup(mms), reads=[wkey] + [("uT", k, th) for k in range(KC)], writes=pk(c0, c0 + 512))
                                if is_gate:
                                    S.op("act", act_(sG3[:, fc, th * 512:(th + 1) * 512], psf(c0, c0 + 512), AF.Silu),
                                         reads=pk(c0, c0 + 512), writes=[("siluG", fc, th)])
                                else:
                                    for hh in range(2):
                                        S.op("act", act_(qT3[hh * 64:(hh + 1) * 64, fc * 2 + hh, th * 512:(th + 1) * 512], psf(c0, c0 + 512)[hh * 64:(hh + 1) * 64, :], AF.Copy),
                                             reads=pk(c0, c0 + 512), writes=[("qTs", fc, th)])
                    S.dma("sp", (lambda e, ho=ho: e.dma_start(out=hg3, in_=ho.ap().rearrange("(r p) f -> p r f", p=128))),
                          reads=[("hout", li, h)], writes=["hg"])
                    for i in range(NCORES):
                        sl = tabs[:, TB_SEL + i:TB_SEL + i + 1]
                        if i == 0:
                            S.op("dve", ts_(hacc, hg3[:, 0, :], sl, None, ALU.mult), reads=["hg", "tabs"], writes=["hacc"])
                        else:
                            S.op("dve", stt_(hacc, hg3[:, i, :], sl, hacc, ALU.mult, ALU.add), reads=["hg", "tabs", "hacc"], writes=["hacc"])
                    S.op("dve", cp_(kTs[:, 0:128], hacc[:, 0:128]), reads=["hacc"], writes=[("kTs", 0)])
                    S.op("dve", cp_(vS3[:, 0, 0:64], hacc[:, 128:192]), reads=["hacc"], writes=["vS0"])
                    def smv(n):
                        s_ = sm[n % 4]
                        return (s_[:, 0:4], s_[:, 4:8], s_[:, 8:12], s_[:, 12:16], s_[:, 16:20], s_[:, 20:24], s_[:, 24:28])

                    def c1(n):
                        tk = slice(n * 128, (n + 1) * 128)
                        sc0 = (3 + 2 * (n % 2)) * 512
                        mms = []
                        for qh in range(4):
                            fc = qh // 2
                            pr = (qh % 2) * 64
                            mms.append((psf(sc0 + qh * 256, sc0 + (qh + 1) * 256), qT3[:, qh, tk], kTs[:, n * 128:n * 128 + 256], True, True))
                        kkeys = [("kTs", 0), ("kTs", 1), ("kTs", 2)]
                        S.op("pe", mmgroup(mms), reads=kkeys + [("qTs", fc, n // 4) for fc in range(2)], writes=pk(sc0, sc0 + 1024))
                        mx, mm_, nm, tmp, es, den, rinv = smv(n)
                        smk = ("sm", n % 4)
                        sc3 = psf(sc0, sc0 + 1024).rearrange("p (q k) -> p q k", q=4)
                        snk = sinks[:, li * 32 + 4 * h:li * 32 + 4 * h + 4]
                        S.op("dve", (lambda e, mx=mx, sc3=sc3: e.tensor_reduce(out=mx, in_=sc3, axis=AX.X, op=ALU.max)),
                             reads=pk(sc0, sc0 + 1024), writes=[smk])
                        S.op("dve", stt_(mm_, mx, 0.125, snk, ALU.mult, ALU.max), reads=[smk, "sinks"], writes=[smk])
                        S.op("dve", ts_(nm, mm_, -1.0, None, ALU.mult), reads=[smk], writes=[smk])
                        S.op("dve", tt_(tmp, snk, nm, ALU.add), reads=[smk, "sinks"], writes=[smk])
                        e_ = E[n % 2]
                        e3 = e_.rearrange("p (q k) -> p q k", q=4)
                        for qh in range(4):
                            S.op("act", act_(e3[:, qh, :], sc3[:, qh, :], AF.Exp, scale=0.125, bias=nm[:, qh:qh + 1]),
                                 reads=pk(sc0 + qh * 256, sc0 + (qh + 1) * 256) + [smk], writes=[("E", n % 2, qh)])
                        S.op("act", act_(es, tmp, AF.Exp), reads=[smk], writes=[("es", n % 4)])
                        p_ = Pm[n % 2]
                        p3 = p_.rearrange("p (q k) -> p q k", q=4)
                        mk = maskA if n == 0 else maskB
                        S.op("pool", tt_(p3, e3, mk[:].unsqueeze(1).broadcast_to([128, 4, 256]), ALU.mult),
                             reads=[("E", n % 2, qh) for qh in range(4)] + ["maskA", "maskB"], writes=[("Pm", n % 2)])

                    def c2(n):
                        p3 = Pm[n % 2].rearrange("p (q k) -> p q k", q=4)
                        ptc = 7 * 512
                        ptp = psb(ptc, ptc + 512)
                        trs = [(ptp[:, (qh * 2 + kt) * 128:(qh * 2 + kt + 1) * 128], p3[:, qh, kt * 128:(kt + 1) * 128], ident_b[:])
                               for qh in range(4) for kt in range(2)]
                        S.op("pe", trgroup(trs), reads=[("Pm", n % 2), "ident_b"], writes=pk(ptc, ptc + 512))
                        pt_ = PTs[n % 2]
                        S.op("act", act_(pt_, ptp, AF.Copy), reads=pk(ptc, ptc + 512), writes=[("PTs", n % 2)])

                    def c3(n):
                        mx, mm_, nm, tmp, es, den, rinv = smv(n)
                        pt_ = PTs[n % 2]
                        pvc = 2 * 512
                        mms = []
                        for qh in range(4):
                            for kt in range(2):
                                mms.append((psf(pvc + qh * 65, pvc + (qh + 1) * 65), pt_[:, (qh * 2 + kt) * 128:(qh * 2 + kt + 1) * 128],
                                            vS3[:, n + kt, 0:65], kt == 0, kt == 1))
                        S.op("pe", mmgroup(mms), reads=[("PTs", n % 2), "vS", "vS0", "vS1"], writes=pk(pvc, pvc + 260))
                        pv3 = psf(pvc, pvc + 260).rearrange("p (q c) -> p q c", q=4)
                        S.op("dve", tt_(den, pv3[:, :, 64], es, ALU.add), reads=pk(pvc, pvc + 260) + [("es", n % 4)], writes=[("den", n % 4)])
                        S.op("dve", (lambda e, rinv=rinv, den=den: e.reciprocal(out=rinv, in_=den)), reads=[("den", n % 4)], writes=[("rinv", n % 4)])
                        a_ = an[n % 2]
                        S.op("dve", tt_(a_.rearrange("p (q c) -> p q c", q=4), pv3[:, :, 0:64], rinv.unsqueeze(2).broadcast_to([128, 4, 64]), ALU.mult),
                             reads=pk(pvc, pvc + 260) + [("rinv", n % 4)], writes=[("an", n % 2)])

                    def c4(n):
                        tk = slice(n * 128, (n + 1) * 128)
                        a_ = an[n % 2]
                        atc = (n % 2) * 512
                        atp = psb(atc, atc + 128)
                        S.op("pe", trgroup([(atp[:, 0:128], a_[:, 0:128], ident_b[:]), (atp[:, 128:256], a_[:, 128:256], ident_b[:])]),
                             reads=[("an", n % 2), "ident_b"], writes=pk(atc, atc + 128))
                        for i in range(2):
                            f = h * 2 + i
                            S.op("dve", tt_(brT[:, f, tk], atp[:, i * 128:(i + 1) * 128], sG3[:, i, tk], ALU.mult),
                                 reads=pk(atc, atc + 128) + [("siluG", i, n // 4)], writes=[("brT", f, n // 4)])
                    for t_ in range(NT + 3):
                        if t_ < NT:
                            c1(t_)
                        if 0 <= t_ - 1 < NT:
                            c2(t_ - 1)
                        if 0 <= t_ - 2 < NT:
                            c3(t_ - 2)
                        if 0 <= t_ - 3 < NT:
                            c4(t_ - 3)
            S.barrier()
            cz = Carve()
            PT = cz.bf(KC * T).rearrange("p (k t) -> p k t", k=KC)
            sig = [cz.bf(512), cz.bf(512)]
            s1 = sbase + (48 if branch == 0 else 96)
            if (branch == 0 and DBG < 3) or (branch == 1 and DBG < 5):
                continue
            for c in range(KC):
                w, wk = slab(s1 + c)
                b0 = (c % 2) * 4
                mmsP, mmsM = [], []
                for k in range(KC):
                    for th in range(2):
                        mmsP.append((psf((b0 + th) * 512, (b0 + th + 1) * 512), w[:, k, 0:128], brT[:, k, th * 512:(th + 1) * 512], k == 0, k == KC - 1))
                for k in range(KC):
                    for th in range(2):
                        mmsM.append((psf((b0 + 2 + th) * 512, (b0 + 3 + th) * 512), w[:, k, 128:256], uT[:, k, th * 512:(th + 1) * 512], k == 0, k == KC - 1))
                S.op("pe", mmgroup(mmsP), reads=[wk] + [("brT", k, th) for k in range(KC) for th in range(2)], writes=pk(b0 * 512, (b0 + 2) * 512))
                S.op("pe", mmgroup(mmsM), reads=[wk] + [("uT", k, th) for k in range(KC) for th in range(2)], writes=pk((b0 + 2) * 512, (b0 + 4) * 512))
                for th in range(2):
                    sg_ = sig[th]
                    S.op("act", act_(sg_, psf((b0 + 2 + th) * 512, (b0 + 3 + th) * 512), AF.Sigmoid),
                         reads=pk((b0 + 2 + th) * 512, (b0 + 3 + th) * 512), writes=[("sig", th)])
                    S.op("dve", tt_(PT[:, c, th * 512:(th + 1) * 512], psf((b0 + th) * 512, (b0 + th + 1) * 512), sg_, ALU.mult),
                         reads=pk((b0 + th) * 512, (b0 + th + 1) * 512) + [("sig", th)], writes=[("PT", c, th)])
            s2 = s1 + 16
            for cp in range(8):
                w, wk = slab(s2 + cp)
                for ci in range(2):
                    c = cp * 2 + ci
                    b0 = (c % 2) * 2
                    mms = []
                    for k in range(KC):
                        for th in range(2):
                            mms.append((psf((b0 + th) * 512, (b0 + th + 1) * 512), w[:, k, ci * 128:(ci + 1) * 128], PT[:, k, th * 512:(th + 1) * 512], k == 0, k == KC - 1))
                    S.op("pe", mmgroup(mms), reads=[wk] + [("PT", k, th) for k in range(KC) for th in range(2)], writes=pk(b0 * 512, (b0 + 2) * 512))
                    for th in range(2):
                        xs = xT[:, c, th * 512:(th + 1) * 512]
                        S.op("dve", stt_(xs, psf((b0 + th) * 512, (b0 + th + 1) * 512), gate[:, c:c + 1], xs, ALU.mult, ALU.add),
                             reads=pk((b0 + th) * 512, (b0 + th + 1) * 512) + ["modsb", ("xT", c, th)], writes=[("xT", c, th)])
            S.barrier()

    if final:
        norm_phase(pv[:, PV_FW:PV_FW + 16], None, False)
        S.barrier()
    cz = Carve()
    stage = [cz.f32(2048), cz.f32(2048)]
    for n in range(NT):
        st = stage[n % 2]
        for g4 in range(4):
            bank = (n * 4 + g4) % 2
            c0 = bank * 512
            trs = [(psf(c0 + j * 128, c0 + (j + 1) * 128), xT[:, g4 * 4 + j, n * 128:(n + 1) * 128], ident_f[:]) for j in range(4)]
            S.op("pe", trgroup(trs), reads=[("xT", g4 * 4 + j, n // 4) for j in range(4)] + ["ident_f"], writes=pk(c0, c0 + 512))
            eng = "act" if g4 % 2 == 0 else "dve"
            outv = st[:, g4 * 512:(g4 + 1) * 512]
            inv = psf(c0, c0 + 512)
            fn = act_(outv, inv, AF.Copy) if eng == "act" else cp_(outv, inv)
            S.op(eng, fn, reads=pk(c0, c0 + 512), writes=[("ostage", n % 2, g4)])
        S.dma("sp", (lambda e, n=n, st=st: e.dma_start(out=y_out[n * 128:(n + 1) * 128, :], in_=st)),
              reads=[("ostage", n % 2, g4) for g4 in range(4)], writes=[("y", n)])
    S.barrier()

    semctx = {}
    for n in S.sem_names:
        g = nc.semaphore("s_" + n)
        ctx.append(g)
        semctx[n] = g.__enter__()
    with nc.Block() as block:
        @block.tensor
        def _(e):
            S.emit("pe", e, semctx)

        @block.scalar
        def _(e):
            S.emit("act", e, semctx)

        @block.vector
        def _(e):
            S.emit("dve", e, semctx)

        @block.gpsimd
        def _(e):
            S.emit("pool", e, semctx)

        @block.sync
        def _(e):
            S.emit("sp", e, semctx)
    for g in reversed(ctx):
        g.__exit__(None, None, None)
    return nc


def _slab(w2d):
    return np.ascontiguousarray(w2d.reshape(KC, 128, 256).transpose(1, 0, 2)).reshape(128, 4096)


def build_slabs(l, ada_w, w_in, w_ret_o, w_swa_o, w_out):
    out = np.empty((NSLAB, 128, 4096), np.float32)
    wi = w_in[l]
    i = 0
    for s in range(24):
        out[i] = _slab(ada_w[l][:, s * 256:(s + 1) * 256]); i += 1
    for h in range(8):
        out[i] = _slab(np.concatenate([wi[:, O_RQ + h * 128:O_RQ + (h + 1) * 128], wi[:, O_RK + h * 128:O_RK + (h + 1) * 128]], axis=1)); i += 1
        out[i] = _slab(wi[:, O_RV + h * 256:O_RV + (h + 1) * 256]); i += 1
        out[i] = _slab(wi[:, O_RG + h * 256:O_RG + (h + 1) * 256]); i += 1
    for c in range(16):
        out[i] = _slab(np.concatenate([w_ret_o[l][:, c * 128:(c + 1) * 128], wi[:, O_MR + c * 128:O_MR + (c + 1) * 128]], axis=1)); i += 1
    for cp in range(8):
        out[i] = _slab(w_out[l][:, cp * 256:(cp + 1) * 256]); i += 1
    for h in range(8):
        kk = wi[:, O_SK + h * 64:O_SK + (h + 1) * 64]
        vv = wi[:, O_SV + h * 64:O_SV + (h + 1) * 64]
        out[i] = _slab(np.concatenate([kk, kk, vv, vv], axis=1)); i += 1
        out[i] = _slab(wi[:, O_SQ + h * 256:O_SQ + (h + 1) * 256]); i += 1
        out[i] = _slab(wi[:, O_SG + h * 256:O_SG + (h + 1) * 256]); i += 1
    for c in range(16):
        out[i] = _slab(np.concatenate([w_swa_o[l][:, c * 128:(c + 1) * 128], wi[:, O_MS + c * 128:O_MS + (c + 1) * 128]], axis=1)); i += 1
    for cp in range(8):
        out[i] = _slab(w_out[l][:, cp * 256:(cp + 1) * 256]); i += 1
    assert i == NSLAB
    return out


def const_tables(core):
    j = core % 4
    b = core // 4
    tabs = np.zeros((128, NTAB), np.float64)
    t = np.arange(128, dtype=np.float64)
    inv = 1.0 / (10000.0 ** np.linspace(0.0, 1.0, 64, dtype=np.float32).astype(np.float64))
    for n in range(8):
        pos = (j * 1024 + n * 128 + t)
        ang = (pos[:, None].astype(np.float32) * inv[None, :].astype(np.float32)).astype(np.float64)
        tabs[:, TB_COS + n * 64:TB_COS + (n + 1) * 64] = np.cos(ang)
        tabs[:, TB_SIN + n * 64:TB_SIN + (n + 1) * 64] = np.sin(ang)
    hh = np.arange(8, dtype=np.float64)
    log_g = np.log1p(-np.exp2(-5.0 - hh))
    tabs[:, TB_XI:TB_XI + 8] = np.exp(log_g[None, :] * (t[:, None] + 1.0))
    tabs[:, TB_KZ:TB_KZ + 8] = np.exp(-log_g[None, :] * (t[:, None] + 1.0)) * (128.0 ** -0.5)
    g = np.exp(log_g * 128.0)
    tabs[:, TB_G:TB_G + 8] = g[None, :]
    for n in range(8):
        tabs[:, TB_GPOW + n * 8:TB_GPOW + (n + 1) * 8] = (g ** n)[None, :]
    for i in range(8):
        if i // 4 == b and (i % 4) < j:
            tabs[:, TB_COEF + i * 8:TB_COEF + (i + 1) * 8] = (g ** (8 * (j - 1 - (i % 4))))[None, :]
        tabs[:, TB_SEL + i] = 1.0 if (i == core - 1 and j > 0) else 0.0
    masks = np.zeros((128, 768), np.float32)
    masks[:, 0:128] = np.eye(128, dtype=np.float32)
    mi = np.arange(128)
    masks[:, 128:256] = (mi[:, None] <= mi[None, :]).astype(np.float32)
    prev = (mi[None, :] > mi[:, None]).astype(np.float32)
    cur = (mi[None, :] <= mi[:, None]).astype(np.float32)
    masks[:, 512:640] = prev
    masks[:, 640:768] = cur
    masks[:, 256:384] = prev if j > 0 else 0.0
    masks[:, 384:512] = cur
    return tabs.astype(np.float32), masks


def fm(v):
    return np.ascontiguousarray(v.reshape(KC, 128).T)


_PROG = {}


def run_layers(xs, l0, l1, final, inputs):
    key = (l1 - l0, final)
    if key not in _PROG:
        _PROG[key] = build_program(0, l1 - l0, final)
    nc = _PROG[key]
    c = inputs["c"]
    wsl = np.concatenate([build_slabs(l, inputs["ada_w"], inputs["w_in"], inputs["w_ret_o"], inputs["w_swa_o"], inputs["w_out"])
                          for l in range(l0, l1)], axis=0)
    pvec = np.zeros((128, NPV), np.float32)
    sinks = np.zeros((128, 128), np.float32)
    for li, l in enumerate(range(l0, l1)):
        pvec[:, PV_NW + li * 16:PV_NW + (li + 1) * 16] = fm(inputs["norm_w"][l])
        for s in range(3):
            pvec[:, PV_AB + li * 48 + s * 16:PV_AB + li * 48 + (s + 1) * 16] = fm(inputs["ada_b"][l][s * D:(s + 1) * D])
        pvec[:, PV_GN + li * 16:PV_GN + (li + 1) * 16] = fm(inputs["ret_gn_w"][l])
        sinks[:, li * 32:(li + 1) * 32] = inputs["attn_sinks"][l][None, :]
    pvec[:, PV_FW:PV_FW + 16] = fm(inputs["final_norm_w"])
    in_maps = []
    for core in range(NCORES):
        tabs, masks = const_tables(core)
        in_maps.append({"x": xs[core], "wslab": wsl, "cvec": fm(c[core // 4]), "pvec": pvec, "sinks": sinks,
                        "tabs": tabs, "masks": masks})
    res = run_bass_kernel_spmd(nc, in_maps, core_ids=list(range(NCORES)))
    return [np.asarray(r["y"]) for r in res.results]


FUSED = True


def kernel(x, c, norm_w, ada_w, ada_b, w_in, ret_gn_w, attn_sinks, w_ret_o, w_swa_o, w_out, final_norm_w):
    inputs = dict(c=np.asarray(c, np.float32), norm_w=np.asarray(norm_w, np.float32), ada_w=np.asarray(ada_w, np.float32),
                  ada_b=np.asarray(ada_b, np.float32), w_in=np.asarray(w_in, np.float32), ret_gn_w=np.asarray(ret_gn_w, np.float32),
                  attn_sinks=np.asarray(attn_sinks, np.float32), w_ret_o=np.asarray(w_ret_o, np.float32),
                  w_swa_o=np.asarray(w_swa_o, np.float32), w_out=np.asarray(w_out, np.float32),
                  final_norm_w=np.asarray(final_norm_w, np.float32))
    x = np.asarray(x, np.float32)
    xs = [np.ascontiguousarray(x[core // 4, (core % 4) * T:(core % 4 + 1) * T, :]) for core in range(NCORES)]
    if FUSED:
        xs = run_layers(xs, 0, DEPTH, True, inputs)
    else:
        for l in range(DEPTH):
            xs = run_layers(xs, l, l + 1, l == DEPTH - 1, inputs)
    out = np.empty((2, 4 * T, D), np.float32)
    for core in range(NCORES):
        out[core // 4, (core % 4) * T:(core % 4 + 1) * T, :] = xs[core]
    return out
```

```python
import numpy as np
import concourse.bass as bass
import concourse.mybir as mybir
from concourse.bass_utils import run_bass_kernel_spmd

F32 = mybir.dt.float32
BF16 = mybir.dt.bfloat16
AF = mybir.ActivationFunctionType
ALU = mybir.AluOpType
AX = mybir.AxisListType

D = 2048
KC = 16
T = 1024
NT = 8
DEPTH = 4
NCORES = 8
EPS = 1e-6
NSLAB = 120
RING = 3
import os
DBG = int(os.environ.get('KDBG', '9'))
O_RQ, O_RK, O_RV, O_RG, O_SQ, O_SK, O_SV, O_SG, O_MR, O_MS = (
    0, 1024, 2048, 4096, 6144, 8192, 8704, 9216, 11264, 13312)

TB_COS, TB_SIN, TB_XI, TB_KZ, TB_G, TB_GPOW, TB_COEF, TB_SEL = 0, 512, 1024, 1032, 1040, 1048, 1112, 1176
NTAB = 1184
PV_NW, PV_AB, PV_GN, PV_FW = 0, 64, 256, 320
NPV = 336


SELF_SKIP = set(os.environ.get('KSELF', 'pe').split(','))


class Sched:
    ENGS = ("pe", "act", "dve", "pool", "sp")

    def __init__(self, ndma_w=8, ndma_m=8):
        self.q = {e: [] for e in self.ENGS}
        self.cnt = {e: 0 for e in self.ENGS}
        self.waited = {e: {} for e in self.ENGS}
        self.lastw = {}
        self.readers = {}
        self.dsem = {"w": ["dw%d" % i for i in range(ndma_w)], "m": ["dm%d" % i for i in range(ndma_m)]}
        self.dcnt = {n: 0 for k in self.dsem for n in self.dsem[k]}
        self.dcnt["cc"] = 0
        self.drr = {"w": 0, "m": 0}
        self.sem_names = list(self.ENGS) + list(self.dcnt.keys())

    def _deps(self, reads, writes):
        toks = []
        for k in reads:
            if k in self.lastw:
                toks.append(self.lastw[k])
        for k in writes:
            if k in self.lastw:
                toks.append(self.lastw[k])
            r = self.readers.get(k)
            if r:
                toks.extend(r.items())
        return toks

    def _waits(self, eng, toks):
        need = {}
        for sem, val in toks:
            if val > need.get(sem, 0):
                need[sem] = val
        w = self.waited[eng]
        for sem, val in need.items():
            if eng == sem and eng in SELF_SKIP:
                continue
            if w.get(sem, 0) >= val:
                continue
            w[sem] = val
            self.q[eng].append(("wait", sem, val))

    def _record(self, tok, reads, writes):
        for k in reads:
            r = self.readers.setdefault(k, {})
            if tok[1] > r.get(tok[0], 0):
                r[tok[0]] = tok[1]
        for k in writes:
            self.lastw[k] = tok
            self.readers[k] = {}

    def op(self, eng, fn, reads=(), writes=()):
        self._waits(eng, self._deps(reads, writes))
        self.cnt[eng] += 1
        tok = (eng, self.cnt[eng])
        self.q[eng].append(("op", fn, eng, 1))
        self._record(tok, reads, writes)
        return tok

    def dma(self, issuer, fn, reads=(), writes=(), kind="m"):
        names = self.dsem[kind]
        ds = names[self.drr[kind] % len(names)]
        self.drr[kind] += 1
        toks = self._deps(reads, writes)
        if self.dcnt[ds] > 0:
            toks.append((ds, self.dcnt[ds]))
        self._waits(issuer, toks)
        self.dcnt[ds] += 16
        tok = (ds, self.dcnt[ds])
        self.q[issuer].append(("op", fn, ds, 16))
        self._record(tok, reads, writes)
        return tok

    def collective(self, fn, reads=(), writes=()):
        toks = self._deps(reads, writes)
        self._waits("pool", toks)
        self.dcnt["cc"] += 1
        tok = ("cc", self.dcnt["cc"])
        self.q["pool"].append(("op", fn, "cc", 1))
        self._record(tok, reads, writes)
        return tok

    def barrier(self):
        toks = [(e, self.cnt[e]) for e in self.ENGS if self.cnt[e] > 0]
        toks += [(n, v) for n, v in self.dcnt.items() if v > 0]
        for e in self.ENGS:
            self._waits(e, [t for t in toks if t[0] != e or e != "pe"])

    def emit(self, eng, handle, sems):
        for it in self.q[eng]:
            if it[0] == "wait":
                handle.wait_ge(sems[it[1]], it[2])
            else:
                inst = it[1](handle)
                inst.then_inc(sems[it[2]], it[3])


def pk(c0, c1):
    return [("ps", q) for q in range(c0 // 512, (c1 + 511) // 512)]


def build_program(l0, l1, final):
    nl = l1 - l0
    nc = bass.Bass("TRN2", target_bir_lowering=False)
    x_in = nc.dram_tensor("x", [T, D], F32, kind="ExternalInput").ap()
    wsl = nc.dram_tensor("wslab", [nl * NSLAB, 128, 4096], F32, kind="ExternalInput").ap()
    cvec = nc.dram_tensor("cvec", [128, KC], F32, kind="ExternalInput").ap()
    pvec = nc.dram_tensor("pvec", [128, NPV], F32, kind="ExternalInput").ap()
    sinks_in = nc.dram_tensor("sinks", [128, 128], F32, kind="ExternalInput").ap()
    tabs_in = nc.dram_tensor("tabs", [128, NTAB], F32, kind="ExternalInput").ap()
    masks_in = nc.dram_tensor("masks", [128, 768], F32, kind="ExternalInput").ap()
    y_out = nc.dram_tensor("y", [T, D], F32, kind="ExternalOutput").ap()
    g_in = [[nc.dram_tensor("gin_%d_%d" % (l, h), [128, 256], F32) for h in range(8)] for l in range(nl)]
    g_out = [[nc.dram_tensor("gout_%d_%d" % (l, h), [NCORES * 128, 256], F32) for h in range(8)] for l in range(nl)]
    h_in = [[nc.dram_tensor("hin_%d_%d" % (l, h), [128, 192], BF16) for h in range(8)] for l in range(nl)]
    h_out = [[nc.dram_tensor("hout_%d_%d" % (l, h), [NCORES * 128, 192], BF16) for h in range(8)] for l in range(nl)]

    S = Sched()
    ctx = []

    def sb(name, shape, dt):
        g = nc.sbuf_tensor("sb_" + name, shape, dt)
        ctx.append(g)
        return g.__enter__()

    xT = sb("xT", [128, KC, T], F32)
    uT = sb("uT", [128, KC, T], BF16)
    brT = sb("brT", [128, KC, T], BF16)
    ring = sb("ring", [128, RING, 4096], BF16)
    PRN = 22144
    PR = sb("PR", [128, PRN], BF16)
    tabs = sb("tabs", [128, NTAB], F32)
    pv = sb("pv", [128, NPV], F32)
    sinks = sb("sinks", [128, 128], F32)
    cv = sb("cv", [128, KC], F32)
    cact = sb("cact", [128, KC], BF16)
    ident_f = sb("ident_f", [128, 128], F32)
    ident_b = sb("ident_b", [128, 128], BF16)
    ones_b = sb("ones_b", [128, 128], BF16)
    causal_b = sb("causal_b", [128, 128], BF16)
    maskA = sb("maskA", [128, 256], BF16)
    maskB = sb("maskB", [128, 256], BF16)
    modsb = sb("modsb", [128, 48], F32)
    asb = sb("asb", [128, KC], F32)
    small = sb("small", [128, 96], F32)
    pg = nc.psum_tensor("pbig", [128, 4096], F32)
    ctx.append(pg)
    PB = pg.__enter__()
    PBh = PB[:, :].bitcast(BF16)

    def psf(c0, c1):
        return PB[:, c0:c1]

    def psb(c0, c1):
        return PBh[:, 2 * c0:2 * c1]

    class Carve:
        def __init__(self):
            self.off = 0

        def bf(self, n):
            a = PR[:, self.off:self.off + n]
            self.off += n
            assert self.off <= PRN, self.off
            return a

        def f32(self, n):
            a = PR[:, self.off:self.off + 2 * n].bitcast(F32)
            self.off += 2 * n
            assert self.off <= PRN, self.off
            return a

    act_ = lambda out, in_, func, **kw: (lambda e: e.activation(out=out, in_=in_, func=func, **kw))
    tt_ = lambda out, a, b, op: (lambda e: e.tensor_tensor(out=out, in0=a, in1=b, op=op))
    ts_ = lambda out, a, s1, s2, op0, op1=None: (
        (lambda e: e.tensor_scalar(out=out, in0=a, scalar1=s1, scalar2=s2, op0=op0, op1=op1)) if op1 is not None
        else (lambda e: e.tensor_scalar(out=out, in0=a, scalar1=s1, scalar2=None, op0=op0)))
    stt_ = lambda out, a, s, b, op0, op1: (lambda e: e.scalar_tensor_tensor(out=out, in0=a, scalar=s, in1=b, op0=op0, op1=op1))
    cp_ = lambda out, in_: (lambda e: e.tensor_copy(out=out, in_=in_))

    def mmgroup(mms):
        def f(pe):
            inst = None
            for (o, l, r, st, sp) in mms:
                inst = pe.matmul(o, lhsT=l, rhs=r, start=st, stop=sp)
            return inst
        return f

    def trgroup(trs):
        def f(pe):
            inst = None
            for (o, i, idn) in trs:
                inst = pe.transpose(o, i, idn)
            return inst
        return f

    wstate = {"issued": 0}
    total_slabs = nl * NSLAB

    def slab_issue_upto(n):
        n = min(n, total_slabs - 1)
        while wstate["issued"] <= n:
            i = wstate["issued"]
            slot = i % RING
            S.dma("pool", (lambda e, i=i, slot=slot: e.dma_start(out=ring[:, slot, :], in_=wsl[i], max_dma_last_dim=8192)),
                  writes=[("ring", slot)], kind="w")
            wstate["issued"] += 1

    def slab(i):
        slab_issue_upto(i + RING - 2)
        slot = i % RING
        return ring[:, slot, :].rearrange("p (k c) -> p k c", k=KC), ("ring", slot)

    S.dma("sp", lambda e: e.dma_start(out=tabs[:], in_=tabs_in), writes=["tabs"])
    S.dma("sp", lambda e: e.dma_start(out=pv[:], in_=pvec), writes=["pv"])
    S.dma("sp", lambda e: e.dma_start(out=sinks[:], in_=sinks_in), writes=["sinks"])
    S.dma("sp", lambda e: e.dma_start(out=cv[:], in_=cvec), writes=["cv"])
    cz = Carve()
    mstage = cz.f32(768)
    stage = [cz.f32(2048), cz.f32(2048)]
    S.dma("sp", lambda e: e.dma_start(out=mstage, in_=masks_in), writes=["mstage"])
    S.op("dve", cp_(ident_f[:], mstage[:, 0:128]), reads=["mstage"], writes=["ident_f"])
    S.op("dve", cp_(ident_b[:], mstage[:, 0:128]), reads=["mstage"], writes=["ident_b"])
    S.op("dve", cp_(causal_b[:], mstage[:, 128:256]), reads=["mstage"], writes=["causal_b"])
    S.op("dve", cp_(maskA[:], mstage[:, 256:512]), reads=["mstage"], writes=["maskA"])
    S.op("dve", cp_(maskB[:], mstage[:, 512:768]), reads=["mstage"], writes=["maskB"])
    S.op("pool", lambda e: e.memset(ones_b[:], 1.0), writes=["ones_b"])
    S.op("pool", lambda e: e.memset(small[:, 0:1], EPS), writes=["epsc"])
    epsc = small[:, 0:1]
    S.op("act", act_(cact[:], cv[:], AF.Silu), reads=["cv"], writes=["cact"])
    slab_issue_upto(RING - 1)

    for n in range(NT):
        st = stage[n % 2]
        S.dma("sp", (lambda e, n=n, st=st: e.dma_start(out=st, in_=x_in[n * 128:(n + 1) * 128, :])),
              writes=[("stage", n % 2)])
        for g4 in range(4):
            bank = (n * 4 + g4) % 2
            c0 = bank * 512
            trs = [(psf(c0 + j * 128, c0 + (j + 1) * 128), st[:, (g4 * 4 + j) * 128:(g4 * 4 + j + 1) * 128], ident_f[:])
                   for j in range(4)]
            S.op("pe", trgroup(trs), reads=[("stage", n % 2), "ident_f"], writes=pk(c0, c0 + 512))
            eng = "act" if g4 % 2 == 0 else "dve"
            outv = xT[:, g4 * 4:(g4 + 1) * 4, n * 128:(n + 1) * 128]
            inv = psf(c0, c0 + 512).rearrange("p (j t) -> p j t", j=4)
            fn = (lambda e, o=outv, i=inv: e.activation(out=o, in_=i, func=AF.Copy)) if eng == "act" else cp_(outv, inv)
            S.op(eng, fn, reads=pk(c0, c0 + 512), writes=[("xT", k, n // 4) for k in range(g4 * 4, g4 * 4 + 4)])
    S.barrier()

    def norm_phase(wcol_ap, with_shift, dst_is_u):
        cz = Carve()
        rstd = cz.f32(1024)
        sq = [cz.bf(512), cz.bf(512)]
        xn = [cz.f32(512), cz.f32(512)]
        for th in range(2):
            c0 = (3 + th) * 512
            for k in range(KC):
                s_ = sq[k % 2]
                S.op("act", act_(s_, xT[:, k, th * 512:(th + 1) * 512], AF.Square), reads=[("xT", k, th)], writes=[("sq", k % 2)])
                S.op("pe", mmgroup([(psf(c0, c0 + 512), ones_b[:], s_, k == 0, k == KC - 1)]),
                     reads=[("sq", k % 2), "ones_b"], writes=pk(c0, c0 + 512))
            rs = rstd[:, th * 512:(th + 1) * 512]
            S.op("act", act_(rs, psf(c0, c0 + 512), AF.Sqrt, scale=1.0 / D, bias=epsc), reads=pk(c0, c0 + 512) + ["epsc"], writes=[("rstd", th)])
            S.op("dve", (lambda e, rs=rs: e.reciprocal(out=rs, in_=rs)), reads=[("rstd", th)], writes=[("rstd", th)])
        i = 0
        for th in range(2):
            for k in range(KC):
                rs = rstd[:, th * 512:(th + 1) * 512]
                xs = xT[:, k, th * 512:(th + 1) * 512]
                t_ = xn[i % 2]
                eng = "dve" if i % 2 == 0 else "pool"
                S.op(eng, tt_(t_, xs, rs, ALU.mult), reads=[("xT", k, th), ("rstd", th)], writes=[("xn", i % 2)])
                if dst_is_u:
                    S.op("act", act_(uT[:, k, th * 512:(th + 1) * 512], t_, AF.Identity, scale=asb[:, k:k + 1], bias=with_shift[:, k:k + 1]),
                         reads=[("xn", i % 2), "asb", "modsb"], writes=[("uT", k, th)])
                else:
                    S.op("act", act_(xs, t_, AF.Identity, scale=wcol_ap[:, k:k + 1]), reads=[("xn", i % 2), "pv"], writes=[("xT", k, th)])
                i += 1

    for li in range(nl):
        l = l0 + li
        sbase = li * NSLAB
        mc0 = 2 * 512
        for i in (range(24) if DBG >= 1 else []):
            w, wk = slab(sbase + i)
            mms = []
            for cc in range(2):
                col = mc0 + i * 2 + cc
                for k in range(KC):
                    mms.append((psf(col, col + 1), w[:, k, cc * 128:(cc + 1) * 128], cact[:, k:k + 1], k == 0, k == KC - 1))
            S.op("pe", mmgroup(mms), reads=[wk, "cact"], writes=pk(mc0, mc0 + 48))
        S.op("dve", tt_(modsb[:], psf(mc0, mc0 + 48), pv[:, PV_AB + li * 48:PV_AB + (li + 1) * 48], ALU.add),
             reads=pk(mc0, mc0 + 48) + ["pv"], writes=["modsb"])
        S.op("dve", stt_(asb[:], modsb[:, 16:32], 1.0, pv[:, PV_NW + li * 16:PV_NW + (li + 1) * 16], ALU.add, ALU.mult),
             reads=["modsb", "pv"], writes=["asb"])
        shift = modsb[:, 0:16]
        gate = modsb[:, 32:48]
        if DBG >= 1:
            norm_phase(None, shift, True)
        S.barrier()

        for branch in range(2):
            if (branch == 0 and DBG < 2) or (branch == 1 and DBG < 4):
                continue
            if branch == 0:
                cz = Carve()
                qkf = [cz.f32(256), cz.f32(256)]
                t1 = cz.f32(256); t2 = cz.f32(256); ssum = cz.f32(256)
                qhat = [cz.bf(128), cz.bf(128)]
                khat = cz.bf(1024)
                vtok2 = [cz.bf(2048), cz.bf(2048)]
                qkT2 = [cz.bf(2048), cz.bf(2048)]
                siluG = cz.bf(2048)
                Sloc = cz.f32(2048)
                Sbf = [cz.bf(256), cz.bf(256)]
                gath = [cz.f32(256), cz.f32(256)]
                Sin = cz.f32(256)
                Gn = cz.f32(256)
                lend = cz.f32(256)
                Pm = [cz.bf(128), cz.bf(128)]
                rhat = [cz.bf(256), cz.bf(256)]
                st6 = [cz.f32(6), cz.f32(6)]
                mv = [cz.f32(2), cz.f32(2)]
                rsr = [cz.f32(2), cz.f32(2)]
                sG3 = siluG.rearrange("p (a t) -> p a t", a=2)
                slabs_a = {}

                def hv(h):
                    return (tabs[:, TB_XI + h:TB_XI + h + 1], tabs[:, TB_KZ + h:TB_KZ + h + 1], tabs[:, TB_G + h:TB_G + h + 1])

                def a_proj(h, n):
                    if (h, "qk") not in slabs_a:
                        slabs_a[(h, "qk")] = slab(sbase + 24 + 3 * h)
                        slabs_a[(h, "v")] = slab(sbase + 24 + 3 * h + 1)
                    wqk, kqk = slabs_a[(h, "qk")]
                    wv, kv_ = slabs_a[(h, "v")]
                    xi, kz, gh = hv(h)
                    vtok = vtok2[h % 2]
                    tk = slice(n * 128, (n + 1) * 128)
                    th = n // 4
                    c0 = (n % 2) * 512
                    mms = [(psf(c0, c0 + 256), uT[:, k, tk], wqk[:, k, :], k == 0, k == KC - 1) for k in range(KC)]
                    mms += [(psf(c0 + 256, c0 + 512), uT[:, k, tk], wv[:, k, :], k == 0, k == KC - 1) for k in range(KC)]
                    S.op("pe", mmgroup(mms), reads=[kqk, kv_] + [("uT", k, th) for k in range(KC)], writes=pk(c0, c0 + 512))
                    qf = qkf[n % 2]
                    S.op("act", act_(qf, psf(c0, c0 + 256), AF.Copy), reads=pk(c0, c0 + 256), writes=[("qkf", n % 2)])
                    S.op("act", act_(vtok[:, n * 256:(n + 1) * 256], psf(c0 + 256, c0 + 512), AF.Copy),
                         reads=pk(c0 + 256, c0 + 512), writes=[("vtok", h % 2, n)])
                    q4 = qf.rearrange("p (a b d) -> p a b d", a=2, b=2)
                    t14 = t1.rearrange("p (a b d) -> p a b d", a=2, b=2)
                    t24 = t2.rearrange("p (a b d) -> p a b d", a=2, b=2)
                    s4 = ssum.rearrange("p (a b d) -> p a b d", a=2, b=2)
                    cosn = tabs[:, TB_COS + n * 64:TB_COS + (n + 1) * 64]
                    sinn = tabs[:, TB_SIN + n * 64:TB_SIN + (n + 1) * 64]
                    cos4 = cosn.unsqueeze(1).unsqueeze(1).broadcast_to([128, 2, 2, 64])
                    sin3 = sinn.unsqueeze(1).broadcast_to([128, 2, 64])
                    S.op("pool", tt_(t14, q4, cos4, ALU.mult), reads=[("qkf", n % 2), "tabs"], writes=["t1"])
                    S.op("pool", tt_(t24[:, :, 0, :], q4[:, :, 1, :], sin3, ALU.mult), reads=[("qkf", n % 2), "tabs"], writes=["t2a"])
                    S.op("pool", tt_(t24[:, :, 1, :], q4[:, :, 0, :], sin3, ALU.mult), reads=[("qkf", n % 2), "tabs"], writes=["t2b"])
                    S.op("pool", tt_(s4[:, :, 0, :], t14[:, :, 0, :], t24[:, :, 0, :], ALU.subtract), reads=["t1", "t2a"], writes=["ssa"])
                    S.op("pool", tt_(s4[:, :, 1, :], t14[:, :, 1, :], t24[:, :, 1, :], ALU.add), reads=["t1", "t2b"], writes=["ssb"])

                def a_proj2(h, n):
                    xi, kz, gh = hv(h)
                    tk = slice(n * 128, (n + 1) * 128)
                    S.op("act", act_(qhat[n % 2], ssum[:, 0:128], AF.Identity, scale=xi), reads=["ssa", "ssb", "tabs"], writes=[("qhat", n % 2)])
                    S.op("act", act_(khat[:, tk], ssum[:, 128:256], AF.Identity, scale=kz), reads=["ssa", "ssb", "tabs"], writes=[("khat", n)])

                def a_tr(h, n):
                    tk = slice(n * 128, (n + 1) * 128)
                    qkT3 = qkT2[h % 2].rearrange("p (a t) -> p a t", a=2)
                    tb = (4 + n % 2) * 512
                    tp = psb(tb, tb + 128)
                    S.op("pe", trgroup([(tp[:, 0:128], qhat[n % 2], ident_b[:]), (tp[:, 128:256], khat[:, tk], ident_b[:])]),
                         reads=[("qhat", n % 2), ("khat", n), "ident_b"], writes=pk(tb, tb + 128))
                    S.op("dve", cp_(qkT3[:, :, tk], tp.rearrange("p (a t) -> p a t", a=2)), reads=pk(tb, tb + 128), writes=[("qkT", h % 2, n)])

                def a_chain_send(h):
                    xi, kz, gh = hv(h)
                    vtok = vtok2[h % 2]
                    for n in range(NT):
                        tk = slice(n * 128, (n + 1) * 128)
                        kc0 = (n % 2) * 512
                        S.op("pe", mmgroup([(psf(kc0, kc0 + 256), khat[:, tk], vtok[:, n * 256:(n + 1) * 256], True, True)]),
                             reads=[("khat", n), ("vtok", h % 2, n)], writes=pk(kc0, kc0 + 256))
                        if n == 0:
                            S.op("dve", cp_(Sloc[:, 0:256], psf(kc0, kc0 + 256)), reads=pk(kc0, kc0 + 256), writes=[("Sloc", 0)])
                        else:
                            S.op("dve", stt_(Sloc[:, n * 256:(n + 1) * 256], Sloc[:, (n - 1) * 256:n * 256], gh, psf(kc0, kc0 + 256), ALU.mult, ALU.add),
                                 reads=pk(kc0, kc0 + 256) + [("Sloc", n - 1), "tabs"], writes=[("Sloc", n)])
                    S.op("act", act_(lend, Sloc[:, 7 * 256:8 * 256], AF.Identity, scale=gh), reads=[("Sloc", 7), "tabs"], writes=["lend"])
                    gi = g_in[li][h]
                    go = g_out[li][h]
                    S.dma("sp", (lambda e, gi=gi: e.dma_start(out=gi.ap(), in_=lend)), reads=["lend"], writes=[("gin", li, h)])
                    S.collective((lambda e, gi=gi, go=go: e.collective_compute(
                        "AllGather", ALU.bypass, replica_groups=[list(range(NCORES))], ins=[gi.ap().opt()], outs=[go.ap().opt()])),
                        reads=[("gin", li, h)], writes=[("gout", li, h)])

                def a_gate(h, fc, th):
                    if (h, "g") not in slabs_a:
                        slabs_a[(h, "g")] = slab(sbase + 24 + 3 * h + 2)
                    wg, kg = slabs_a[(h, "g")]
                    gc0 = (2 + (fc * 2 + th) % 2) * 512
                    mms = [(psf(gc0, gc0 + 512), wg[:, k, fc * 128:(fc + 1) * 128], uT[:, k, th * 512:(th + 1) * 512], k == 0, k == KC - 1)
                           for k in range(KC)]
                    S.op("pe", mmgroup(mms), reads=[kg] + [("uT", k, th) for k in range(KC)], writes=pk(gc0, gc0 + 512))
                    S.op("act", act_(sG3[:, fc, th * 512:(th + 1) * 512], psf(gc0, gc0 + 512), AF.Silu),
                         reads=pk(gc0, gc0 + 512), writes=[("siluG", fc, th)])

                def a_sin(h):
                    go = g_out[li][h]
                    for i in range(NCORES):
                        gb = gath[i % 2]
                        S.dma("sp", (lambda e, go=go, gb=gb, i=i: e.dma_start(out=gb, in_=go.ap()[i * 128:(i + 1) * 128, :])),
                              reads=[("gout", li, h)], writes=[("gath", i % 2)])
                        cf = tabs[:, TB_COEF + i * 8 + h:TB_COEF + i * 8 + h + 1]
                        if i == 0:
                            S.op("dve", ts_(Sin, gb, cf, None, ALU.mult), reads=[("gath", 0), "tabs"], writes=["Sin"])
                        else:
                            S.op("dve", stt_(Sin, gb, cf, Sin, ALU.mult, ALU.add), reads=[("gath", i % 2), "tabs", "Sin"], writes=["Sin"])

                def b1(h, n):
                    xi, kz, gh = hv(h)
                    qkT3 = qkT2[h % 2].rearrange("p (a t) -> p a t", a=2)
                    tk = slice(n * 128, (n + 1) * 128)
                    sb_ = Sbf[n % 2]
                    if n == 0:
                        S.op("act", act_(sb_, Sin, AF.Copy), reads=["Sin"], writes=[("Sbf", 0)])
                    else:
                        gp = tabs[:, TB_GPOW + n * 8 + h:TB_GPOW + n * 8 + h + 1]
                        S.op("act", act_(Gn, Sin, AF.Identity, scale=gp), reads=["Sin", "tabs"], writes=["Gn"])
                        S.op("dve", stt_(sb_, Sloc[:, (n - 1) * 256:n * 256], gh, Gn, ALU.mult, ALU.add),
                             reads=[("Sloc", n - 1), "Gn", "tabs"], writes=[("Sbf", n % 2)])
                    sc0 = (6 + n % 2) * 512 + 256
                    S.op("pe", mmgroup([(psf(sc0, sc0 + 128), qkT3[:, 1, tk], qkT3[:, 0, tk], True, True)]),
                         reads=[("qkT", h % 2, n)], writes=pk(sc0, sc0 + 128))
                    S.op("dve", tt_(Pm[n % 2], psf(sc0, sc0 + 128), causal_b[:], ALU.mult), reads=pk(sc0, sc0 + 128) + ["causal_b"], writes=[("Pm", n % 2)])

                def b2(h, n):
                    qkT3 = qkT2[h % 2].rearrange("p (a t) -> p a t", a=2)
                    vtok = vtok2[h % 2]
                    tk = slice(n * 128, (n + 1) * 128)
                    sb_ = Sbf[n % 2]
                    pm_ = Pm[n % 2]
                    oc0 = (6 + (n + 1) % 2) * 512
                    S.op("pe", mmgroup([(psf(oc0, oc0 + 256), pm_, vtok[:, n * 256:(n + 1) * 256], True, False),
                                        (psf(oc0, oc0 + 256), qkT3[:, 0, tk], sb_, False, True)]),
                         reads=[("Pm", n % 2), ("vtok", h % 2, n), ("qkT", h % 2, n), ("Sbf", n % 2)], writes=pk(oc0, oc0 + 256))
                    s6 = st6[n % 2]; m2 = mv[n % 2]; r1 = rsr[n % 2]; rh = rhat[n % 2]
                    S.op("dve", (lambda e, s6=s6, oc0=oc0: e.bn_stats(out=s6, in_=psf(oc0, oc0 + 256))), reads=pk(oc0, oc0 + 256), writes=[("st6", n % 2)])
                    S.op("dve", (lambda e, s6=s6, m2=m2: e.bn_aggr(out=m2, in_=s6)), reads=[("st6", n % 2)], writes=[("mv", n % 2)])

                def b2b(h, n):
                    m2 = mv[n % 2]; r1 = rsr[n % 2]
                    S.op("act", act_(r1[:, 0:1], m2[:, 1:2], AF.Sqrt, scale=1.0, bias=epsc), reads=[("mv", n % 2), "epsc"], writes=[("rsr", n % 2)])
                    S.op("dve", (lambda e, r1=r1: e.reciprocal(out=r1[:, 0:1], in_=r1[:, 0:1])), reads=[("rsr", n % 2)], writes=[("rsr", n % 2)])
                    S.op("dve", stt_(r1[:, 1:2], m2[:, 0:1], -1.0, r1[:, 0:1], ALU.mult, ALU.mult), reads=[("mv", n % 2), ("rsr", n % 2)], writes=[("nmr", n % 2)])

                def b2c(h, n):
                    r1 = rsr[n % 2]; rh = rhat[n % 2]
                    oc0 = (6 + (n + 1) % 2) * 512
                    S.op("act", act_(rh, psf(oc0, oc0 + 256), AF.Identity, scale=r1[:, 0:1], bias=r1[:, 1:2]),
                         reads=pk(oc0, oc0 + 256) + [("nmr", n % 2), ("rsr", n % 2)], writes=[("rhat", n % 2)])

                def b3(h, n):
                    tk = slice(n * 128, (n + 1) * 128)
                    rh = rhat[n % 2]
                    rb = (2 + n % 2) * 512
                    rp = psb(rb, rb + 128)
                    S.op("pe", trgroup([(rp[:, 0:128], rh[:, 0:128], ident_b[:]), (rp[:, 128:256], rh[:, 128:256], ident_b[:])]),
                         reads=[("rhat", n % 2), "ident_b"], writes=pk(rb, rb + 128))
                    for i in range(2):
                        f = h * 2 + i
                        gw = pv[:, PV_GN + li * 16 + f:PV_GN + li * 16 + f + 1]
                        S.op("dve", stt_(brT[:, f, tk], rp[:, i * 128:(i + 1) * 128], gw, sG3[:, i, tk], ALU.mult, ALU.mult),
                             reads=pk(rb, rb + 128) + [("siluG", i, n // 4), "pv"], writes=[("brT", f, n // 4)])

                for t_ in range(NT + 2):
                    if 0 <= t_ - 1 < NT:
                        a_proj2(0, t_ - 1)
                    if 0 <= t_ - 2 < NT:
                        a_tr(0, t_ - 2)
                    if t_ < NT:
                        a_proj(0, t_)
                a_chain_send(0)
                for fc in range(2):
                    for th in range(2):
                        a_gate(0, fc, th)
                for h in range(8):
                    a_sin(h)
                    for t_ in range(15):
                        if 0 <= t_ - 6 < NT:
                            b3(h, t_ - 6)
                        if 0 <= t_ - 5 < NT:
                            b2c(h, t_ - 5)
                        if 0 <= t_ - 4 < NT:
                            b2b(h, t_ - 4)
                        if 0 <= t_ - 3 < NT:
                            b2(h, t_ - 3)
                        if 0 <= t_ - 2 < NT:
                            b1(h, t_ - 2)
                        if h < 7:
                            if 0 <= t_ - 1 < NT:
                                a_proj2(h + 1, t_ - 1)
                            if 0 <= t_ - 2 < NT:
                                a_tr(h + 1, t_ - 2)
                            if t_ < NT:
                                a_proj(h + 1, t_)
                            if t_ == 10:
                                a_chain_send(h + 1)
                                a_gate(h + 1, 0, 0)
                            if t_ == 11:
                                a_gate(h + 1, 1, 0)
                            if t_ == 13:
                                a_gate(h + 1, 0, 1)
                            if t_ == 14:
                                a_gate(h + 1, 1, 1)
            else:
                cz = Carve()
                kTs = cz.bf(1152)
                vS = cz.bf(9 * 66)
                qTs = cz.bf(4096)
                siluG = cz.bf(2048)
                hg = cz.bf(8 * 192)
                hacc = cz.f32(192)
                E = [cz.bf(1024), cz.bf(1024)]
                Pm = [cz.bf(1024), cz.bf(1024)]
                PTs = [cz.bf(1024), cz.bf(1024)]
                an = [cz.bf(256), cz.bf(256)]
                sm = [cz.f32(32), cz.f32(32), cz.f32(32), cz.f32(32)]
                vS3 = vS.rearrange("p (b c) -> p b c", c=66)
                qT3 = qTs.rearrange("p (a t) -> p a t", a=4)
                sG3 = siluG.rearrange("p (a t) -> p a t", a=2)
                hg3 = hg.rearrange("p (r c) -> p r c", c=192)
                S.op("pool", lambda e: e.memset(vS3[:, :, 64:66], 1.0), writes=["vS1"])
                S.op("pool", lambda e: e.memset(qTs, 0.0), writes=[("qTs", fc, th) for fc in range(2) for th in range(2)])
                for h in range(8):
                    wkv, kkv = slab(sbase + 72 + 3 * h)
                    for th in range(2):
                        c0 = (th % 2) * 512
                        mms = [(psf(c0, c0 + 512), wkv[:, k, 0:128], uT[:, k, th * 512:(th + 1) * 512], k == 0, k == KC - 1) for k in range(KC)]
                        S.op("pe", mmgroup(mms), reads=[kkv] + [("uT", k, th) for k in range(KC)], writes=pk(c0, c0 + 512))
                        S.op("act", act_(kTs[:, 128 + th * 512:128 + (th + 1) * 512], psf(c0, c0 + 512), AF.Copy),
                             reads=pk(c0, c0 + 512), writes=[("kTs", 1 + th)])
                    vc0 = 2 * 512
                    mms = []
                    for n in range(NT):
                        mms += [(psf(vc0 + n * 64, vc0 + (n + 1) * 64), uT[:, k, n * 128:(n + 1) * 128], wkv[:, k, 128:192], k == 0, k == KC - 1)
                                for k in range(KC)]
                    S.op("pe", mmgroup(mms), reads=[kkv] + [("uT", k, th) for k in range(KC) for th in range(2)], writes=pk(vc0, vc0 + 512))
                    S.op("dve", cp_(vS3[:, 1:9, 0:64], psf(vc0, vc0 + 512).rearrange("p (b c) -> p b c", c=64)),
                         reads=pk(vc0, vc0 + 512), writes=["vS"])
                    hi = h_in[li][h]
                    ho = h_out[li][h]
                    S.dma("sp", (lambda e, hi=hi: e.dma_start(out=hi.ap()[:, 0:128], in_=kTs[:, 1024:1152])), reads=[("kTs", 2)], writes=[("hin", li, h, 0)])
                    S.dma("sp", (lambda e, hi=hi: e.dma_start(out=hi.ap()[:, 128:192], in_=vS3[:, 8, 0:64])), reads=["vS"], writes=[("hin", li, h, 1)])
                    S.collective((lambda e, hi=hi, ho=ho: e.collective_compute(
                        "AllGather", ALU.bypass, replica_groups=[list(range(NCORES))], ins=[hi.ap().opt()], outs=[ho.ap().opt()])),
                        reads=[("hin", li, h, 0), ("hin", li, h, 1)], writes=[("hout", li, h)])
                    for is_gate in (False, True):
                        wsrc, wkey = slab(sbase + 72 + 3 * h + (2 if is_gate else 1))
                        for fc in range(2):
                            for th in range(2):
                                c0 = ((fc * 2 + th) % 2) * 512
                                mms = [(psf(c0, c0 + 512), wsrc[:, k, fc * 128:(fc + 1) * 128], uT[:, k, th * 512:(th + 1) * 512], k == 0, k == KC - 1)
                                       for k in range(KC)]
                                S.op("pe", mmgroup(mms), reads=[wkey] + [("uT", k, th) for k in range(KC)], writes=pk(c0, c0 + 512))
                                if is_gate:
                                    S.op("act", act_(sG3[:, fc, th * 512:(th + 1) * 512], psf(c0, c0 + 512), AF.Silu),
                                         reads=pk(c0, c0 + 512), writes=[("siluG", fc, th)])
                                else:
                                    for hh in range(2):
                                        S.op("act", act_(qT3[hh * 64:(hh + 1) * 64, fc * 2 + hh, th * 512:(th + 1) * 512], psf(c0, c0 + 512)[hh * 64:(hh + 1) * 64, :], AF.Copy),
                                             reads=pk(c0, c0 + 512), writes=[("qTs", fc, th)])
                    S.dma("sp", (lambda e, ho=ho: e.dma_start(out=hg3, in_=ho.ap().rearrange("(r p) f -> p r f", p=128))),
                          reads=[("hout", li, h)], writes=["hg"])
                    for i in range(NCORES):
                        sl = tabs[:, TB_SEL + i:TB_SEL + i + 1]
                        if i == 0:
                            S.op("dve", ts_(hacc, hg3[:, 0, :], sl, None, ALU.mult), reads=["hg", "tabs"], writes=["hacc"])
                        else:
                            S.op("dve", stt_(hacc, hg3[:, i, :], sl, hacc, ALU.mult, ALU.add), reads=["hg", "tabs", "hacc"], writes=["hacc"])
                    S.op("dve", cp_(kTs[:, 0:128], hacc[:, 0:128]), reads=["hacc"], writes=[("kTs", 0)])
                    S.op("dve", cp_(vS3[:, 0, 0:64], hacc[:, 128:192]), reads=["hacc"], writes=["vS0"])
                    def smv(n):
                        s_ = sm[n % 4]
                        return (s_[:, 0:4], s_[:, 4:8], s_[:, 8:12], s_[:, 12:16], s_[:, 16:20], s_[:, 20:24], s_[:, 24:28])

                    def c1(n):
                        tk = slice(n * 128, (n + 1) * 128)
                        sc0 = (3 + 2 * (n % 2)) * 512
                        mms = []
                        for qh in range(4):
                            fc = qh // 2
                            pr = (qh % 2) * 64
                            mms.append((psf(sc0 + qh * 256, sc0 + (qh + 1) * 256), qT3[:, qh, tk], kTs[:, n * 128:n * 128 + 256], True, True))
                        kkeys = [("kTs", 0), ("kTs", 1), ("kTs", 2)]
                        S.op("pe", mmgroup(mms), reads=kkeys + [("qTs", fc, n // 4) for fc in range(2)], writes=pk(sc0, sc0 + 1024))
                        mx, mm_, nm, tmp, es, den, rinv = smv(n)
                        smk = ("sm", n % 4)
                        sc3 = psf(sc0, sc0 + 1024).rearrange("p (q k) -> p q k", q=4)
                        snk = sinks[:, li * 32 + 4 * h:li * 32 + 4 * h + 4]
                        S.op("dve", (lambda e, mx=mx, sc3=sc3: e.tensor_reduce(out=mx, in_=sc3, axis=AX.X, op=ALU.max)),
                             reads=pk(sc0, sc0 + 1024), writes=[smk])
                        S.op("dve", stt_(mm_, mx, 0.125, snk, ALU.mult, ALU.max), reads=[smk, "sinks"], writes=[smk])
                        S.op("dve", ts_(nm, mm_, -1.0, None, ALU.mult), reads=[smk], writes=[smk])
                        S.op("dve", tt_(tmp, snk, nm, ALU.add), reads=[smk, "sinks"], writes=[smk])
                        e_ = E[n % 2]
                        e3 = e_.rearrange("p (q k) -> p q k", q=4)
                        for qh in range(4):
                            S.op("act", act_(e3[:, qh, :], sc3[:, qh, :], AF.Exp, scale=0.125, bias=nm[:, qh:qh + 1]),
                                 reads=pk(sc0 + qh * 256, sc0 + (qh + 1) * 256) + [smk], writes=[("E", n % 2, qh)])
                        S.op("act", act_(es, tmp, AF.Exp), reads=[smk], writes=[("es", n % 4)])
                        p_ = Pm[n % 2]
                        p3 = p_.rearrange("p (q k) -> p q k", q=4)
                        mk = maskA if n == 0 else maskB
                        S.op("pool", tt_(p3, e3, mk[:].unsqueeze(1).broadcast_to([128, 4, 256]), ALU.mult),
                             reads=[("E", n % 2, qh) for qh in range(4)] + ["maskA", "maskB"], writes=[("Pm", n % 2)])

                    def c2(n):
                        p3 = Pm[n % 2].rearrange("p (q k) -> p q k", q=4)
                        ptc = 7 * 512
                        ptp = psb(ptc, ptc + 512)
                        trs = [(ptp[:, (qh * 2 + kt) * 128:(qh * 2 + kt + 1) * 128], p3[:, qh, kt * 128:(kt + 1) * 128], ident_b[:])
                               for qh in range(4) for kt in range(2)]
                        S.op("pe", trgroup(trs), reads=[("Pm", n % 2), "ident_b"], writes=pk(ptc, ptc + 512))
                        pt_ = PTs[n % 2]
                        S.op("act", act_(pt_, ptp, AF.Copy), reads=pk(ptc, ptc + 512), writes=[("PTs", n % 2)])

                    def c3(n):
                        mx, mm_, nm, tmp, es, den, rinv = smv(n)
                        pt_ = PTs[n % 2]
                        pvc = 2 * 512
                        mms = []
                        for qh in range(4):
                            for kt in range(2):
                                mms.append((psf(pvc + qh * 65, pvc + (qh + 1) * 65), pt_[:, (qh * 2 + kt) * 128:(qh * 2 + kt + 1) * 128],
                                            vS3[:, n + kt, 0:65], kt == 0, kt == 1))
                        S.op("pe", mmgroup(mms), reads=[("PTs", n % 2), "vS", "vS0", "vS1"], writes=pk(pvc, pvc + 260))
                        pv3 = psf(pvc, pvc + 260).rearrange("p (q c) -> p q c", q=4)
                        S.op("dve", tt_(den, pv3[:, :, 64], es, ALU.add), reads=pk(pvc, pvc + 260) + [("es", n % 4)], writes=[("den", n % 4)])
                        S.op("dve", (lambda e, rinv=rinv, den=den: e.reciprocal(out=rinv, in_=den)), reads=[("den", n % 4)], writes=[("rinv", n % 4)])
                        a_ = an[n % 2]
                        S.op("dve", tt_(a_.rearrange("p (q c) -> p q c", q=4), pv3[:, :, 0:64], rinv.unsqueeze(2).broadcast_to([128, 4, 64]), ALU.mult),
                             reads=pk(pvc, pvc + 260) + [("rinv", n % 4)], writes=[("an", n % 2)])

                    def c4(n):
                        tk = slice(n * 128, (n + 1) * 128)
                        a_ = an[n % 2]
                        atc = (n % 2) * 512
                        atp = psb(atc, atc + 128)
                        S.op("pe", trgroup([(atp[:, 0:128], a_[:, 0:128], ident_b[:]), (atp[:, 128:256], a_[:, 128:256], ident_b[:])]),
                             reads=[("an", n % 2), "ident_b"], writes=pk(atc, atc + 128))
                        for i in range(2):
                            f = h * 2 + i
                            S.op("dve", tt_(brT[:, f, tk], atp[:, i * 128:(i + 1) * 128], sG3[:, i, tk], ALU.mult),
                                 reads=pk(atc, atc + 128) + [("siluG", i, n // 4)], writes=[("brT", f, n // 4)])
                    for t_ in range(NT + 3):
                        if t_ < NT:
                            c1(t_)
                        if 0 <= t_ - 1 < NT:
                            c2(t_ - 1)
                        if 0 <= t_ - 2 < NT:
                            c3(t_ - 2)
                        if 0 <= t_ - 3 < NT:
                            c4(t_ - 3)
            S.barrier()
            cz = Carve()
            PT = cz.bf(KC * T).rearrange("p (k t) -> p k t", k=KC)
            sig = [cz.bf(512), cz.bf(512)]
            s1 = sbase + (48 if branch == 0 else 96)
            if (branch == 0 and DBG < 3) or (branch == 1 and DBG < 5):
                continue
            for c in range(KC):
                w, wk = slab(s1 + c)
                b0 = (c % 2) * 4
                mmsP, mmsM = [], []
                for k in range(KC):
                    for th in range(2):
                        mmsP.append((psf((b0 + th) * 512, (b0 + th + 1) * 512), w[:, k, 0:128], brT[:, k, th * 512:(th + 1) * 512], k == 0, k == KC - 1))
                for k in range(KC):
                    for th in range(2):
                        mmsM.append((psf((b0 + 2 + th) * 512, (b0 + 3 + th) * 512), w[:, k, 128:256], uT[:, k, th * 512:(th + 1) * 512], k == 0, k == KC - 1))
                S.op("pe", mmgroup(mmsP), reads=[wk] + [("brT", k, th) for k in range(KC) for th in range(2)], writes=pk(b0 * 512, (b0 + 2) * 512))
                S.op("pe", mmgroup(mmsM), reads=[wk] + [("uT", k, th) for k in range(KC) for th in range(2)], writes=pk((b0 + 2) * 512, (b0 + 4) * 512))
                for th in range(2):
                    sg_ = sig[th]
                    S.op("act", act_(sg_, psf((b0 + 2 + th) * 512, (b0 + 3 + th) * 512), AF.Sigmoid),
                         reads=pk((b0 + 2 + th) * 512, (b0 + 3 + th) * 512), writes=[("sig", th)])
                    S.op("dve", tt_(PT[:, c, th * 512:(th + 1) * 512], psf((b0 + th) * 512, (b0 + th + 1) * 512), sg_, ALU.mult),
                         reads=pk((b0 + th) * 512, (b0 + th + 1) * 512) + [("sig", th)], writes=[("PT", c, th)])
            s2 = s1 + 16
            for cp in range(8):
                w, wk = slab(s2 + cp)
                for ci in range(2):
                    c = cp * 2 + ci
                    b0 = (c % 2) * 2
                    mms = []
                    for k in range(KC):
                        for th in range(2):
                            mms.append((psf((b0 + th) * 512, (b0 + th + 1) * 512), w[:, k, ci * 128:(ci + 1) * 128], PT[:, k, th * 512:(th + 1) * 512], k == 0, k == KC - 1))
                    S.op("pe", mmgroup(mms), reads=[wk] + [("PT", k, th) for k in range(KC) for th in range(2)], writes=pk(b0 * 512, (b0 + 2) * 512))
                    for th in range(2):
                        xs = xT[:, c, th * 512:(th + 1) * 512]
                        S.op("dve", stt_(xs, psf((b0 + th) * 512, (b0 + th + 1) * 512), gate[:, c:c + 1], xs, ALU.mult, ALU.add),
                             reads=pk((b0 + th) * 512, (b0 + th + 1) * 512) + ["modsb", ("xT", c, th)], writes=[("xT", c, th)])
            S.barrier()

    if final:
        norm_phase(pv[:, PV_FW:PV_FW + 16], None, False)
        S.barrier()
    cz = Carve()
    stage = [cz.f32(2048), cz.f32(2048)]
    for n in range(NT):
        st = stage[n % 2]
        for g4 in range(4):
            bank = (n * 4 + g4) % 2
            c0 = bank * 512
            trs = [(psf(c0 + j * 128, c0 + (j + 1) * 128), xT[:, g4 * 4 + j, n * 128:(n + 1) * 128], ident_f[:]) for j in range(4)]
            S.op("pe", trgroup(trs), reads=[("xT", g4 * 4 + j, n // 4) for j in range(4)] + ["ident_f"], writes=pk(c0, c0 + 512))
            eng = "act" if g4 % 2 == 0 else "dve"
            outv = st[:, g4 * 512:(g4 + 1) * 512]
            inv = psf(c0, c0 + 512)
            fn = act_(outv, inv, AF.Copy) if eng == "act" else cp_(outv, inv)
            S.op(eng, fn, reads=pk(c0, c0 + 512), writes=[("ostage", n % 2, g4)])
        S.dma("sp", (lambda e, n=n, st=st: e.dma_start(out=y_out[n * 128:(n + 1) * 128, :], in_=st)),
              reads=[("ostage", n % 2, g4) for g4 in range(4)], writes=[("y", n)])
    S.barrier()

    semctx = {}
    for n in S.sem_names:
        g = nc.semaphore("s_" + n)
        ctx.append(g)
        semctx[n] = g.__enter__()
    with nc.Block() as block:
        @block.tensor
        def _(e):
            S.emit("pe", e, semctx)

        @block.scalar
        def _(e):
            S.emit("act", e, semctx)

        @block.vector
        def _(e):
            S.emit("dve", e, semctx)

        @block.gpsimd
        def _(e):
            S.emit("pool", e, semctx)

        @block.sync
        def _(e):
            S.emit("sp", e, semctx)
    for g in reversed(ctx):
        g.__exit__(None, None, None)
    return nc


def _slab(w2d):
    return np.ascontiguousarray(w2d.reshape(KC, 128, 256).transpose(1, 0, 2)).reshape(128, 4096)


def build_slabs(l, ada_w, w_in, w_ret_o, w_swa_o, w_out):
    out = np.empty((NSLAB, 128, 4096), np.float32)
    wi = w_in[l]
    i = 0
    for s in range(24):
        out[i] = _slab(ada_w[l][:, s * 256:(s + 1) * 256]); i += 1
    for h in range(8):
        out[i] = _slab(np.concatenate([wi[:, O_RQ + h * 128:O_RQ + (h + 1) * 128], wi[:, O_RK + h * 128:O_RK + (h + 1) * 128]], axis=1)); i += 1
        out[i] = _slab(wi[:, O_RV + h * 256:O_RV + (h + 1) * 256]); i += 1
        out[i] = _slab(wi[:, O_RG + h * 256:O_RG + (h + 1) * 256]); i += 1
    for c in range(16):
        out[i] = _slab(np.concatenate([w_ret_o[l][:, c * 128:(c + 1) * 128], wi[:, O_MR + c * 128:O_MR + (c + 1) * 128]], axis=1)); i += 1
    for cp in range(8):
        out[i] = _slab(w_out[l][:, cp * 256:(cp + 1) * 256]); i += 1
    for h in range(8):
        kk = wi[:, O_SK + h * 64:O_SK + (h + 1) * 64]
        vv = wi[:, O_SV + h * 64:O_SV + (h + 1) * 64]
        out[i] = _slab(np.concatenate([kk, kk, vv, vv], axis=1)); i += 1
        out[i] = _slab(wi[:, O_SQ + h * 256:O_SQ + (h + 1) * 256]); i += 1
        out[i] = _slab(wi[:, O_SG + h * 256:O_SG + (h + 1) * 256]); i += 1
    for c in range(16):
        out[i] = _slab(np.concatenate([w_swa_o[l][:, c * 128:(c + 1) * 128], wi[:, O_MS + c * 128:O_MS + (c + 1) * 128]], axis=1)); i += 1
    for cp in range(8):
        out[i] = _slab(w_out[l][:, cp * 256:(cp + 1) * 256]); i += 1
    assert i == NSLAB
    return out


def const_tables(core):
    j = core % 4
    b = core // 4
    tabs = np.zeros((128, NTAB), np.float64)
    t = np.arange(128, dtype=np.float64)
    inv = 1.0 / (10000.0 ** np.linspace(0.0, 1.0, 64, dtype=np.float32).astype(np.float64))
    for n in range(8):
        pos = (j * 1024 + n * 128 + t)
        ang = (pos[:, None].astype(np.float32) * inv[None, :].astype(np.float32)).astype(np.float64)
        tabs[:, TB_COS + n * 64:TB_COS + (n + 1) * 64] = np.cos(ang)
        tabs[:, TB_SIN + n * 64:TB_SIN + (n + 1) * 64] = np.sin(ang)
    hh = np.arange(8, dtype=np.float64)
    log_g = np.log1p(-np.exp2(-5.0 - hh))
    tabs[:, TB_XI:TB_XI + 8] = np.exp(log_g[None, :] * (t[:, None] + 1.0))
    tabs[:, TB_KZ:TB_KZ + 8] = np.exp(-log_g[None, :] * (t[:, None] + 1.0)) * (128.0 ** -0.5)
    g = np.exp(log_g * 128.0)
    tabs[:, TB_G:TB_G + 8] = g[None, :]
    for n in range(8):
        tabs[:, TB_GPOW + n * 8:TB_GPOW + (n + 1) * 8] = (g ** n)[None, :]
    for i in range(8):
        if i // 4 == b and (i % 4) < j:
            tabs[:, TB_COEF + i * 8:TB_COEF + (i + 1) * 8] = (g ** (8 * (j - 1 - (i % 4))))[None, :]
        tabs[:, TB_SEL + i] = 1.0 if (i == core - 1 and j > 0) else 0.0
    masks = np.zeros((128, 768), np.float32)
    masks[:, 0:128] = np.eye(128, dtype=np.float32)
    mi = np.arange(128)
    masks[:, 128:256] = (mi[:, None] <= mi[None, :]).astype(np.float32)
    prev = (mi[None, :] > mi[:, None]).astype(np.float32)
    cur = (mi[None, :] <= mi[:, None]).astype(np.float32)
    masks[:, 512:640] = prev
    masks[:, 640:768] = cur
    masks[:, 256:384] = prev if j > 0 else 0.0
    masks[:, 384:512] = cur
    return tabs.astype(np.float32), masks


def fm(v):
    return np.ascontiguousarray(v.reshape(KC, 128).T)


_PROG = {}


def run_layers(xs, l0, l1, final, inputs):
    key = (l1 - l0, final)
    if key not in _PROG:
        _PROG[key] = build_program(0, l1 - l0, final)
    nc = _PROG[key]
    c = inputs["c"]
    wsl = np.concatenate([build_slabs(l, inputs["ada_w"], inputs["w_in"], inputs["w_ret_o"], inputs["w_swa_o"], inputs["w_out"])
                          for l in range(l0, l1)], axis=0)
    pvec = np.zeros((128, NPV), np.float32)
    sinks = np.zeros((128, 128), np.float32)
    for li, l in enumerate(range(l0, l1)):
        pvec[:, PV_NW + li * 16:PV_NW + (li + 1) * 16] = fm(inputs["norm_w"][l])
        for s in range(3):
            pvec[:, PV_AB + li * 48 + s * 16:PV_AB + li * 48 + (s + 1) * 16] = fm(inputs["ada_b"][l][s * D:(s + 1) * D])
        pvec[:, PV_GN + li * 16:PV_GN + (li + 1) * 16] = fm(inputs["ret_gn_w"][l])
        sinks[:, li * 32:(li + 1) * 32] = inputs["attn_sinks"][l][None, :]
    pvec[:, PV_FW:PV_FW + 16] = fm(inputs["final_norm_w"])
    in_maps = []
    for core in range(NCORES):
        tabs, masks = const_tables(core)
        in_maps.append({"x": xs[core], "wslab": wsl, "cvec": fm(c[core // 4]), "pvec": pvec, "sinks": sinks,
                        "tabs": tabs, "masks": masks})
    res = run_bass_kernel_spmd(nc, in_maps, core_ids=list(range(NCORES)))
    return [np.asarray(r["y"]) for r in res.results]


FUSED = True


def kernel(x, c, norm_w, ada_w, ada_b, w_in, ret_gn_w, attn_sinks, w_ret_o, w_swa_o, w_out, final_norm_w):
    inputs = dict(c=np.asarray(c, np.float32), norm_w=np.asarray(norm_w, np.float32), ada_w=np.asarray(ada_w, np.float32),
                  ada_b=np.asarray(ada_b, np.float32), w_in=np.asarray(w_in, np.float32), ret_gn_w=np.asarray(ret_gn_w, np.float32),
                  attn_sinks=np.asarray(attn_sinks, np.float32), w_ret_o=np.asarray(w_ret_o, np.float32),
                  w_swa_o=np.asarray(w_swa_o, np.float32), w_out=np.asarray(w_out, np.float32),
                  final_norm_w=np.asarray(final_norm_w, np.float32))
    x = np.asarray(x, np.float32)
    xs = [np.ascontiguousarray(x[core // 4, (core % 4) * T:(core % 4 + 1) * T, :]) for core in range(NCORES)]
    if FUSED:
        xs = run_layers(xs, 0, DEPTH, True, inputs)
    else:
        for l in range(DEPTH):
            xs = run_layers(xs, l, l + 1, l == DEPTH - 1, inputs)
    out = np.empty((2, 4 * T, D), np.float32)
    for core in range(NCORES):
        out[core // 4, (core % 4) * T:(core % 4 + 1) * T, :] = xs[core]
    return out
```
